# Optimizing a Trainium2 kernel written in Bass

```python
import jax, jax.numpy as jnp
from jax import lax
import numpy as np

D_MODEL = 1024
BATCH = 8
SEQ = 4096
DEPTH = 1

CONV_WIDTH_A = D_MODEL // 2
CONV_K = 3
N_HEADS = 8
HEAD_DIM = 64
N_KV_GROUPS = 2
HEADS_PER_GROUP = N_HEADS // N_KV_GROUPS
ATTN_WIDTH = N_HEADS * HEAD_DIM
KV_WIDTH = N_KV_GROUPS * HEAD_DIM
CMP_BLOCK = 32
CMP_STRIDE = 16
SEL_BLOCK = 64
N_SEL = 16
WINDOW = 512
Q_BLOCK = 128
N_NSA_BRANCH = 3
ROPE_THETA = 10000.0
D_FF = 2816
N_MERGE = 2
EPS = 1e-6
NEG = -1e30
FORCE = 1e9

SPLIT_SIZES = [CONV_WIDTH_A, CONV_WIDTH_A, CONV_WIDTH_A, ATTN_WIDTH] + [KV_WIDTH] * 6 + [N_NSA_BRANCH * N_HEADS, N_MERGE * D_MODEL]
IN_COLS = sum(SPLIT_SIZES)

kernel_name = "hybrid_shortconv_nsa_convffn_block"


def rmsnorm(x, g):
    x32 = x.astype(jnp.float32)
    y = x32 * lax.rsqrt(jnp.mean(x32 * x32, axis=-1, keepdims=True) + EPS)
    return (y * g.astype(jnp.float32)).astype(x.dtype)


def causal_dwconv(u, w):
    s = u.shape[1]
    up = jnp.pad(u, ((0, 0), (CONV_K - 1, 0), (0, 0)))
    y = w[0] * up[:, 0:s]
    for k in range(1, CONV_K):
        y = y + w[k] * up[:, k:k + s]
    return y


def rope(x, positions):
    half = HEAD_DIM // 2
    inv = ROPE_THETA ** (-jnp.arange(half, dtype=jnp.float32) / half)
    ang = positions.astype(jnp.float32)[..., None] * inv
    cos = jnp.cos(ang)[:, :, None, :]
    sin = jnp.sin(ang)[:, :, None, :]
    x32 = x.astype(jnp.float32)
    x1, x2 = x32[..., :half], x32[..., half:]
    out = jnp.concatenate([x1 * cos - x2 * sin, x2 * cos + x1 * sin], axis=-1)
    return out.astype(x.dtype)


def compress(kv, pe, w1, w2):
    b, s = kv.shape[0], kv.shape[1]
    n_cmp = (s - CMP_BLOCK) // CMP_STRIDE + 1
    idx = np.arange(n_cmp)[:, None] * CMP_STRIDE + np.arange(CMP_BLOCK)[None, :]
    blk = kv[:, idx] + pe[None, None, :, None, :]
    blk = blk.transpose(0, 1, 3, 2, 4).reshape(b, n_cmp, N_KV_GROUPS, CMP_BLOCK * HEAD_DIM)
    return jax.nn.silu(blk @ w1) @ w2


def masked_softmax(s, mask):
    p = jax.nn.softmax(jnp.where(mask, s, NEG), axis=-1)
    return jnp.where(mask, p, 0.0)


def nsa(q, k_cmp, v_cmp, k_sel, v_sel, k_win, v_win, gates):
    b, s = q.shape[0], q.shape[1]
    n_cmp = k_cmp.shape[1]
    n_blk = s // SEL_BLOCK
    n_sel = min(N_SEL, n_blk)
    n_qb = s // Q_BLOCK
    scale = HEAD_DIM ** -0.5
    cmp_end = jnp.arange(n_cmp) * CMP_STRIDE + CMP_BLOCK - 1
    ii = np.arange(n_cmp)[:, None] * CMP_STRIDE
    jj = np.arange(n_blk)[None, :] * SEL_BLOCK
    overlap = jnp.asarray(((ii < jj + SEL_BLOCK) & (ii + CMP_BLOCK > jj)).astype(np.float32))
    ksb = k_sel.reshape(b, n_blk, SEL_BLOCK, N_KV_GROUPS, HEAD_DIM).transpose(0, 3, 1, 2, 4)
    vsb = v_sel.reshape(b, n_blk, SEL_BLOCK, N_KV_GROUPS, HEAD_DIM).transpose(0, 3, 1, 2, 4)
    kwp = jnp.pad(k_win, ((0, 0), (WINDOW, 0), (0, 0), (0, 0)))
    vwp = jnp.pad(v_win, ((0, 0), (WINDOW, 0), (0, 0), (0, 0)))
    b_ix = jnp.arange(b)[:, None, None, None]
    g_ix = jnp.arange(N_KV_GROUPS)[None, :, None, None]
    blk_ids = jnp.arange(n_blk)

    def query_block(qb):
        t0 = qb * Q_BLOCK
        t = t0 + jnp.arange(Q_BLOCK)
        qc = lax.dynamic_slice_in_dim(q, t0, Q_BLOCK, axis=1).reshape(b, Q_BLOCK, N_KV_GROUPS, HEADS_PER_GROUP, HEAD_DIM)
        gc = lax.dynamic_slice_in_dim(gates, t0, Q_BLOCK, axis=1).reshape(b, Q_BLOCK, N_KV_GROUPS, HEADS_PER_GROUP, N_NSA_BRANCH)
        sc = jnp.einsum('bqghd,bkgd->bghqk', qc, k_cmp).astype(jnp.float32) * scale
        p_c = masked_softmax(sc, cmp_end[None, :] <= t[:, None])
        o_c = jnp.einsum('bghqk,bkgd->bqghd', p_c.astype(v_cmp.dtype), v_cmp)
        imp = jnp.einsum('bghqk,kj->bgqj', p_c, overlap)
        cur = t // SEL_BLOCK
        forced = (blk_ids[None, :] == 0) | (blk_ids[None, :] == cur[:, None]) | (blk_ids[None, :] == cur[:, None] - 1)
        valid = blk_ids[None, :] <= cur[:, None]
        imp = jnp.where(forced, FORCE, jnp.where(valid, imp, -FORCE))
        _, sel = lax.top_k(imp, n_sel)
        ks = ksb[b_ix, g_ix, sel].reshape(b, N_KV_GROUPS, Q_BLOCK, n_sel * SEL_BLOCK, HEAD_DIM)
        vs = vsb[b_ix, g_ix, sel].reshape(b, N_KV_GROUPS, Q_BLOCK, n_sel * SEL_BLOCK, HEAD_DIM)
        kpos = (sel[..., None] * SEL_BLOCK + jnp.arange(SEL_BLOCK)).reshape(b, N_KV_GROUPS, Q_BLOCK, n_sel * SEL_BLOCK)
        ss = jnp.einsum('bqghd,bgqkd->bghqk', qc, ks).astype(jnp.float32) * scale
        p_s = masked_softmax(ss, (kpos <= t[None, None, :, None])[:, :, None])
        o_s = jnp.einsum('bghqk,bgqkd->bqghd', p_s.astype(vs.dtype), vs)
        kw = lax.dynamic_slice_in_dim(kwp, t0, WINDOW + Q_BLOCK, axis=1)
        vw = lax.dynamic_slice_in_dim(vwp, t0, WINDOW + Q_BLOCK, axis=1)
        spos = t0 - WINDOW + jnp.arange(WINDOW + Q_BLOCK)
        diff = t[:, None] - spos[None, :]
        mask_w = (diff >= 0) & (diff < WINDOW) & (spos[None, :] >= 0)
        sw = jnp.einsum('bqghd,bkgd->bghqk', qc, kw).astype(jnp.float32) * scale
        p_w = masked_softmax(sw, mask_w)
        o_w = jnp.einsum('bghqk,bkgd->bqghd', p_w.astype(vw.dtype), vw)
        out = gc[..., 0:1] * o_c + gc[..., 1:2] * o_s + gc[..., 2:3] * o_w
        return out.reshape(b, Q_BLOCK, ATTN_WIDTH)

    out = lax.map(query_block, jnp.arange(n_qb))
    return out.transpose(1, 0, 2, 3).reshape(b, s, ATTN_WIDTH)


def setup_inputs(seed: int = 0) -> dict:
    key = jax.random.key(seed)
    ks = jax.random.split(key, 24)
    L = DEPTH
    nrm = lambda k, shape, scale: jax.random.normal(k, shape, jnp.float32) * scale
    x = jax.random.normal(ks[0], (BATCH, SEQ, D_MODEL), jnp.float32)
    positions = (jnp.arange(SEQ, dtype=jnp.int32)[None, :] + jax.random.randint(ks[1], (BATCH, 1), 0, 1024, dtype=jnp.int32)).astype(jnp.int32)
    return {
        'x': x,
        'positions': positions,
        'norm1_g': 1.0 + nrm(ks[2], (L, D_MODEL), 0.02),
        'w_in': nrm(ks[3], (L, D_MODEL, IN_COLS), D_MODEL ** -0.5),
        'conv_a_w': nrm(ks[4], (L, CONV_K, CONV_WIDTH_A), CONV_K ** -0.5),
        'w_a_out': nrm(ks[5], (L, CONV_WIDTH_A, D_MODEL), CONV_WIDTH_A ** -0.5),
        'q_norm_g': 1.0 + nrm(ks[6], (L, HEAD_DIM), 0.02),
        'k_norm_g': 1.0 + nrm(ks[7], (L, N_NSA_BRANCH, HEAD_DIM), 0.02),
        'cmp_k_pe': nrm(ks[8], (L, CMP_BLOCK, HEAD_DIM), 0.1),
        'cmp_k_w1': nrm(ks[9], (L, CMP_BLOCK * HEAD_DIM, HEAD_DIM), (CMP_BLOCK * HEAD_DIM) ** -0.5),
        'cmp_k_w2': nrm(ks[10], (L, HEAD_DIM, HEAD_DIM), HEAD_DIM ** -0.5),
        'cmp_v_pe': nrm(ks[11], (L, CMP_BLOCK, HEAD_DIM), 0.1),
        'cmp_v_w1': nrm(ks[12], (L, CMP_BLOCK * HEAD_DIM, HEAD_DIM), (CMP_BLOCK * HEAD_DIM) ** -0.5),
        'cmp_v_w2': nrm(ks[13], (L, HEAD_DIM, HEAD_DIM), HEAD_DIM ** -0.5),
        'w_b_out': nrm(ks[14], (L, ATTN_WIDTH, D_MODEL), ATTN_WIDTH ** -0.5),
        'w_o': nrm(ks[15], (L, D_MODEL, D_MODEL), D_MODEL ** -0.5),
        'norm2_g': 1.0 + nrm(ks[16], (L, D_MODEL), 0.02),
        'w_ffn_in': nrm(ks[17], (L, D_MODEL, 2 * D_FF), D_MODEL ** -0.5),
        'ffn_conv_w': nrm(ks[18], (L, CONV_K, D_FF), CONV_K ** -0.5),
        'ffn_conv_b': nrm(ks[19], (L, D_FF), 0.02),
        'w_ffn_out': nrm(ks[20], (L, D_FF, D_MODEL), D_FF ** -0.5),
    }


def reference(x, positions, norm1_g, w_in, conv_a_w, w_a_out, q_norm_g, k_norm_g, cmp_k_pe, cmp_k_w1, cmp_k_w2, cmp_v_pe, cmp_v_w1, cmp_v_w2, w_b_out, w_o, norm2_g, w_ffn_in, ffn_conv_w, ffn_conv_b, w_ffn_out):
    b, s, _ = x.shape
    cuts = list(np.cumsum(SPLIT_SIZES)[:-1].tolist())
    for l in range(DEPTH):
        h = rmsnorm(x, norm1_g[l])
        proj = h @ w_in[l]
        a_b, a_c, a_x, q, kc, vc, ksl, vsl, kwn, vwn, g_nsa, g_merge = jnp.split(proj, cuts, axis=-1)
        y_a = a_b * causal_dwconv(a_c * a_x, conv_a_w[l])
        q = rope(rmsnorm(q.reshape(b, s, N_HEADS, HEAD_DIM), q_norm_g[l]), positions)
        kv_shape = (b, s, N_KV_GROUPS, HEAD_DIM)
        k_cmp = rmsnorm(compress(kc.reshape(kv_shape), cmp_k_pe[l], cmp_k_w1[l], cmp_k_w2[l]), k_norm_g[l, 0])
        v_cmp = compress(vc.reshape(kv_shape), cmp_v_pe[l], cmp_v_w1[l], cmp_v_w2[l])
        k_sel = rope(rmsnorm(ksl.reshape(kv_shape), k_norm_g[l, 1]), positions)
        k_win = rope(rmsnorm(kwn.reshape(kv_shape), k_norm_g[l, 2]), positions)
        gates = jax.nn.sigmoid(g_nsa).reshape(b, s, N_HEADS, N_NSA_BRANCH)
        y_b = nsa(q, k_cmp, v_cmp, k_sel, vsl.reshape(kv_shape), k_win, vwn.reshape(kv_shape), gates)
        gm = jax.nn.sigmoid(g_merge)
        mix = gm[..., :D_MODEL] * (y_a @ w_a_out[l]) + gm[..., D_MODEL:] * (y_b @ w_b_out[l])
        x = x + mix @ w_o[l]
        h = rmsnorm(x, norm2_g[l])
        gu = h @ w_ffn_in[l]
        g = causal_dwconv(gu[..., :D_FF], ffn_conv_w[l]) + ffn_conv_b[l]
        x = x + (jax.nn.silu(g) * gu[..., D_FF:]) @ w_ffn_out[l]
    return x
```

```python
from contextlib import ExitStack
import math
import numpy as np
import concourse.bass as bass
import concourse.mybir as mybir
from concourse.bass_utils import run_bass_kernel_spmd

F32 = mybir.dt.float32
BF16 = mybir.dt.bfloat16
I32 = mybir.dt.int32
AF = mybir.ActivationFunctionType
ALU = mybir.AluOpType

EPOCH = 12000
N_DMA_SEMS = 8
S = 4096
D = 1024
T = 512
NTILES = S // T
DFF = 2816
NFC = DFF // 128
EPS = 1e-6
NEGB = -30000.0
NR = 8
import os
SYNC_SAME = os.environ.get("KSYNC", "1") == "1"


class Op:
    __slots__ = ("eng", "fn", "raw", "oth", "dma", "need_inc", "sem", "val", "idx")

    def __init__(self, eng, fn, dma):
        self.eng = eng
        self.fn = fn
        self.dma = dma
        self.raw = set()
        self.oth = set()
        self.need_inc = False
        self.sem = None
        self.val = 0


class Prog:
    COMPUTE = ("pe", "act", "dve", "pool")
    ALL = ("pe", "act", "dve", "pool", "sp")

    def __init__(self, nc):
        self.nc = nc
        self.ops = []
        self.last_w = {}
        self.readers = {}
        self.stack = ExitStack()
        self.n_sb = 0

    def sb(self, shape, dtype, name=None):
        self.n_sb += 1
        name = (name or f"sb{self.n_sb}") + "_sb"
        return self.stack.enter_context(self.nc.sbuf_tensor(name, list(shape), dtype))

    def ps(self, shape, dtype, name=None):
        self.n_sb += 1
        name = name or f"ps{self.n_sb}"
        return self.stack.enter_context(self.nc.psum_tensor(name, list(shape), dtype))

    def add(self, eng, fn, reads=(), writes=(), dma=False):
        op = Op(eng, fn, dma)
        for k in reads:
            w = self.last_w.get(k)
            if w is not None:
                op.raw.add(w)
        for k in writes:
            w = self.last_w.get(k)
            if w is not None:
                op.oth.add(w)
            for r in self.readers.get(k, ()):
                op.oth.add(r)
        for k in writes:
            self.last_w[k] = op
            self.readers[k] = []
        for k in reads:
            if k not in writes:
                self.readers.setdefault(k, []).append(op)
        op.idx = len(self.ops)
        self.ops.append(op)
        return op

    def pe(self, fn, reads=(), writes=()):
        return self.add("pe", fn, reads, writes)

    def act(self, fn, reads=(), writes=()):
        return self.add("act", fn, reads, writes)

    def dve(self, fn, reads=(), writes=()):
        return self.add("dve", fn, reads, writes)

    def pool(self, fn, reads=(), writes=()):
        return self.add("pool", fn, reads, writes)

    def dma(self, eng, out, in_, reads=(), writes=()):
        return self.add(eng, lambda e: e.dma_start(out=out, in_=in_), reads, writes, dma=True)

    def emit(self, final_wait_ops=()):
        nc = self.nc
        ops = self.ops
        deps_of = []
        dma_count = {e: 0 for e in self.ALL}
        dma_prev = {}
        for op in ops:
            deps = set()
            for d in op.raw:
                if d.eng == op.eng and not d.dma and not op.dma and op.eng == "pe":
                    continue
                deps.add(d)
            for d in op.oth:
                if d.eng == op.eng and not d.dma and not op.dma and (op.eng == "pe" or not SYNC_SAME):
                    continue
                deps.add(d)
            if op.dma:
                slot = dma_count[op.eng] % N_DMA_SEMS
                dma_count[op.eng] += 1
                p = dma_prev.get((op.eng, slot))
                if p is not None:
                    deps.add(p)
                dma_prev[(op.eng, slot)] = op
                op.sem = ("dma", op.eng, slot)
                op.need_inc = True
            for d in deps:
                d.need_inc = True
            deps_of.append(deps)
        self.add("sp", None)
        fdeps = set(final_wait_ops)
        for d in fdeps:
            d.need_inc = True
        deps_of.append(fdeps)
        cnt = {e: 0 for e in self.COMPUTE}
        dcnt = {}
        sem_names = set()
        for op in ops:
            if not op.need_inc:
                continue
            if op.dma:
                dcnt[op.sem] = dcnt.get(op.sem, 0) + 16
                op.val = dcnt[op.sem]
            else:
                c = cnt[op.eng]
                op.sem = ("c", op.eng, c // EPOCH)
                op.val = c % EPOCH + 1
                cnt[op.eng] = c + 1
            sem_names.add(op.sem)
        sems = {}
        for s in sorted(sem_names):
            sems[s] = self.stack.enter_context(nc.semaphore("s_" + "_".join(map(str, s))))
        per_eng = {e: [] for e in self.ALL}
        for op, deps in zip(ops, deps_of):
            per_eng[op.eng].append((op, deps))
        n_wait = [0]

        def run(eng_name, e):
            known = {}
            for op, deps in per_eng[eng_name]:
                need = {}
                for d in deps:
                    if d.val > need.get(d.sem, 0):
                        need[d.sem] = d.val
                for s, v in need.items():
                    if known.get(s, 0) >= v:
                        continue
                    e.wait_ge(sems[s], v)
                    n_wait[0] += 1
                    known[s] = v
                if op.fn is None:
                    continue
                ins = op.fn(e)
                if op.need_inc:
                    ins.then_inc(sems[op.sem], 16 if op.dma else 1)

        with nc.Block() as block:
            @block.tensor
            def _(e):
                run("pe", e)

            @block.scalar
            def _(e):
                run("act", e)

            @block.vector
            def _(e):
                run("dve", e)

            @block.gpsimd
            def _(e):
                run("pool", e)

            @block.sync
            def _(e):
                run("sp", e)
        self.stats = dict(n_ops=len(ops), n_inc=sum(1 for o in ops if o.need_inc), n_sems=len(sems), n_wait=n_wait[0],
                          per_eng={k: len(v) for k, v in per_eng.items()})

    def close(self):
        self.stack.close()


SPLIT = [512, 512, 512, 512] + [128] * 6 + [24, 2048]
OFF = np.concatenate([[0], np.cumsum(SPLIT)]).tolist()
O_AB, O_AC, O_AX, O_Q, O_KC, O_VC, O_KSL, O_VSL, O_KWN, O_VWN, O_GN, O_GM = OFF[:12]

CH_NAMES = []
for ci in range(4):
    CH_NAMES += [f"ac{ci}", f"ax{ci}", f"ab{ci}"]
CH_NAMES += [f"q{c}" for c in range(4)] + ["ksl", "kwn", "kc", "vc", "vsl", "vwn"]
CH_NAMES += ["gtT"]
for dc in range(8):
    CH_NAMES += [f"wab{dc}", f"gmA{dc}", f"gmB{dc}"]
CH_NAMES += [f"wo{h}_{kq}" for h in range(2) for kq in range(4)]
for fc in range(NFC):
    CH_NAMES += [f"fg{fc}", f"fu{fc}"]
CH_NAMES += [f"wd{fc}" for fc in range(NFC)]
CH = {n: i for i, n in enumerate(CH_NAMES)}
NCH = len(CH_NAMES)
NCHP = (NCH + 3) // 4 * 4

PP_CA = 0
PP_GQ = 12
PP_GK = 13
PP_FW = 16
PP_FB = 82
PP_INV = 104
PP_SGN = 105
PP_G1 = 106
PP_G2 = 114
PP_EPS = 122
PP_TINY = 123
PP_ZERO = 124
NPP = 126

C_ID, C_PERM, C_ONES, C_W2K, C_W2V, C_WM, C_WB, C_PEK, C_PEV, C_OVL = 0, 128, 256, 384, 512, 640, 768, 896, 928, 960
NC128 = 960 + 130


def _chunk_cols(w, cols):
    sub = w[:, cols]
    return sub.reshape(8, 128, len(cols)).transpose(1, 0, 2).reshape(128, 8 * len(cols))


def host_layout(inp):
    L = 0
    w_in = np.asarray(inp["w_in"][L], np.float32)
    chunks = np.zeros((NCHP, 128, 1024), np.float32)

    def put(name, arr):
        chunks[CH[name]] = arr

    ar = np.arange
    for ci in range(4):
        put(f"ac{ci}", _chunk_cols(w_in, O_AC + ci * 128 + ar(128)))
        put(f"ax{ci}", _chunk_cols(w_in, O_AX + ci * 128 + ar(128)))
        put(f"ab{ci}", _chunk_cols(w_in, O_AB + ci * 128 + ar(128)))
        put(f"q{ci}", _chunk_cols(w_in, O_Q + ci * 128 + ar(128)))
    for n, o in (("ksl", O_KSL), ("kwn", O_KWN), ("kc", O_KC), ("vc", O_VC), ("vsl", O_VSL), ("vwn", O_VWN)):
        put(n, _chunk_cols(w_in, o + ar(128)))
    gt = np.zeros((128, 8, 128), np.float32)
    gt[:, :, 0:24] = _chunk_cols(w_in, O_GN + ar(24)).reshape(128, 8, 24)
    put("gtT", gt.reshape(128, 1024))
    w_a = np.asarray(inp["w_a_out"][L], np.float32)
    w_b = np.asarray(inp["w_b_out"][L], np.float32)
    for dc in range(8):
        a = w_a[:, dc * 128:(dc + 1) * 128].reshape(4, 128, 128).transpose(1, 0, 2)
        b = w_b[:, dc * 128:(dc + 1) * 128].reshape(4, 128, 128).transpose(1, 0, 2)
        put(f"wab{dc}", np.concatenate([a, b], axis=1).reshape(128, 1024))
        put(f"gmA{dc}", _chunk_cols(w_in, O_GM + dc * 128 + ar(128)))
        put(f"gmB{dc}", _chunk_cols(w_in, O_GM + 1024 + dc * 128 + ar(128)))
    w_o = np.asarray(inp["w_o"][L], np.float32)
    for h in range(2):
        for kq in range(4):
            blk = w_o[kq * 256:(kq + 1) * 256, h * 512:(h + 1) * 512]
            put(f"wo{h}_{kq}", blk.reshape(2, 128, 512).transpose(1, 0, 2).reshape(128, 1024))
    w_f = np.asarray(inp["w_ffn_in"][L], np.float32)
    w_d = np.asarray(inp["w_ffn_out"][L], np.float32)
    for fc in range(NFC):
        put(f"fg{fc}", _chunk_cols(w_f, fc * 128 + ar(128)))
        put(f"fu{fc}", _chunk_cols(w_f, DFF + fc * 128 + ar(128)))
        put(f"wd{fc}", w_d[fc * 128:(fc + 1) * 128, :])
    pp = np.zeros((128, NPP), np.float32)
    ca = np.asarray(inp["conv_a_w"][L], np.float32)
    pp[:, PP_CA:PP_CA + 12] = ca.reshape(3, 4, 128).transpose(2, 1, 0).reshape(128, 12)
    pp[:, PP_GQ] = np.tile(np.asarray(inp["q_norm_g"][L], np.float32), 2)
    kg = np.asarray(inp["k_norm_g"][L], np.float32)
    for i in range(3):
        pp[:, PP_GK + i] = np.tile(kg[i], 2)
    fw = np.asarray(inp["ffn_conv_w"][L], np.float32)
    pp[:, PP_FW:PP_FW + 66] = fw.reshape(3, NFC, 128).transpose(2, 1, 0).reshape(128, 66)
    pp[:, PP_FB:PP_FB + 22] = np.asarray(inp["ffn_conv_b"][L], np.float32).reshape(NFC, 128).T
    inv = (10000.0 ** (-np.arange(32, dtype=np.float32) / 32)).astype(np.float32)
    pp[:, PP_INV] = np.tile(inv, 4)
    pp[:, PP_SGN] = np.tile(np.concatenate([-np.ones(32), np.ones(32)]), 2)
    pp[:, PP_G1:PP_G1 + 8] = np.asarray(inp["norm1_g"][L], np.float32).reshape(8, 128).T
    pp[:, PP_G2:PP_G2 + 8] = np.asarray(inp["norm2_g"][L], np.float32).reshape(8, 128).T
    pp[:, PP_EPS] = EPS
    pp[:, PP_TINY] = 1e-20
    pp[:, PP_ZERO] = 0.0
    cw = np.zeros((2, 128, 32, 128), np.float32)
    for i, nm in enumerate(("cmp_k_w1", "cmp_v_w1")):
        w1 = np.asarray(inp[nm][L], np.float32).reshape(32, 64, 64)
        cw[i, 0:64, :, 0:64] = w1.transpose(1, 0, 2)
        cw[i, 64:128, :, 64:128] = w1.transpose(1, 0, 2)
    c128 = np.zeros((128, NC128), np.float32)
    c128[:, C_ID:C_ID + 128] = np.eye(128)
    perm = np.zeros((128, 128), np.float32)
    for m in range(128):
        k = (m // 64) * 64 + ((m % 64) + 32) % 64
        perm[k, m] = 1.0
    c128[:, C_PERM:C_PERM + 128] = perm
    ones = np.zeros((128, 128), np.float32)
    ones[0:64, 0:64] = 1.0
    ones[64:128, 64:128] = 1.0
    c128[:, C_ONES:C_ONES + 128] = ones
    for c0, nm in ((C_W2K, "cmp_k_w2"), (C_W2V, "cmp_v_w2")):
        w2 = np.asarray(inp[nm][L], np.float32)
        c128[0:64, c0:c0 + 64] = w2
        c128[64:128, c0 + 64:c0 + 128] = w2
    hi = (np.arange(128) >= 64).astype(np.int64)[:, None]
    kk = np.arange(128)[None, :] - 64
    c128[:, C_WM:C_WM + 128] = (kk <= hi - 2)
    wb = np.zeros((128, 128), np.float32)
    wb[(kk == hi) | (kk == hi - 1)] = 1e9
    wb[kk > hi] = -1e9
    c128[:, C_WB:C_WB + 128] = wb
    c128[:, C_PEK:C_PEK + 32] = np.tile(np.asarray(inp["cmp_k_pe"][L], np.float32).T, (2, 1))
    c128[:, C_PEV:C_PEV + 32] = np.tile(np.asarray(inp["cmp_v_pe"][L], np.float32).T, (2, 1))
    ovl = np.zeros((2, 128, 65), np.float32)
    for s in range(1, 256):
        i = s - 1
        for j in range(64):
            if 16 * i < 64 * j + 64 and 16 * i + 32 > 64 * j:
                ovl[s // 128, s % 128, j] = 1.0
        ovl[s // 128, s % 128, 64] = 1.0
    c128[:, C_OVL:C_OVL + 130] = ovl.transpose(1, 0, 2).reshape(128, 130)
    p = np.arange(128)[:, None]
    k = np.arange(896)[None, :]
    caus = np.where(p <= k - 384, 0.0, NEGB).astype(np.float32)
    winlo = np.where(p > k - 384, 0.0, NEGB).astype(np.float32)
    k2 = np.arange(2048)[None, :]
    cmpm = np.where(k2 >= 16 * p + 15, 0.0, NEGB).astype(np.float32)
    pj = np.arange(128)[None, :]
    tri_le = np.where(p <= pj, 0.0, NEGB).astype(np.float32)
    tri_gt = np.where(p > pj, 0.0, NEGB).astype(np.float32)
    masks = np.concatenate([caus, winlo, cmpm, tri_le, tri_gt], axis=1)
    expd = (np.arange(4096)[None, :] // 64 == np.arange(64)[:, None]).astype(np.float32)
    gb = np.stack([np.broadcast_to(np.asarray(inp["norm1_g"][L], np.float32)[None, :], (128, D)),
                   np.broadcast_to(np.asarray(inp["norm2_g"][L], np.float32)[None, :], (128, D))]).copy()
    selg = np.zeros((24, 24, 64), np.float32)
    for r in range(24):
        selg[r, r, :] = 1.0
    return dict(selg=selg.reshape(24, 1536), gb=gb, wch=chunks, pp=pp, cw=cw.reshape(2, 128, 4096), c128=c128, masks=masks, expd=expd)


def build(NT=NTILES, dbg=()):
    nc = bass.Bass("TRN2", target_bir_lowering=False)
    x_d = nc.dram_tensor("x", [S, D], F32, kind="ExternalInput").ap()
    pos_d = nc.dram_tensor("pos", [1, S], I32, kind="ExternalInput").ap()
    wch_d = nc.dram_tensor("wch", [NCHP, 128, 1024], F32, kind="ExternalInput").ap()
    pp_d = nc.dram_tensor("pp", [128, NPP], F32, kind="ExternalInput").ap()
    cw_d = nc.dram_tensor("cw", [2, 128, 4096], F32, kind="ExternalInput").ap()
    c128_d = nc.dram_tensor("c128", [128, NC128], F32, kind="ExternalInput").ap()
    mask_d = nc.dram_tensor("masks", [128, 4096], F32, kind="ExternalInput").ap()
    exp_d = nc.dram_tensor("expd", [64, 4096], F32, kind="ExternalInput").ap()
    gb_d = nc.dram_tensor("gb", [2, 128, D], F32, kind="ExternalInput").ap()
    selg_d = nc.dram_tensor("selg", [24, 1536], F32, kind="ExternalInput").ap()
    out_d = nc.dram_tensor("out", [S, D], F32, kind="ExternalOutput").ap()
    wsc = nc.dram_tensor("wsc", [NCHP, 128, 1024], BF16, kind="Internal").ap()
    dbg_d = {n: nc.dram_tensor("dbg_" + n, [128, 512], F32, kind="ExternalOutput").ap() for n in dbg}

    P = Prog(nc)
    sb = P.sb
    pp = sb([128, NPP], F32, "pp")
    c32 = sb([128, NC128], F32, "c32")
    cbf = sb([128, NC128], BF16, "cbf")
    masks = sb([128, 2048 + 256], BF16, "masksb")
    w1bd = sb([128, 2, 4096], BF16, "w1bd")
    kselst = sb([128, 2, S], BF16, "kselst")
    vsel = sb([128, 32, 2, 128], BF16, "vsel")
    kwinT = sb([128, 2, 1024], BF16, "kwinT")
    vwin = sb([128, 8, 2, 128], BF16, "vwin")
    kcmpT = sb([128, 2, 256], BF16, "kcmpT")
    vcmp = sb([128, 2, 2, 128], BF16, "vcmp")
    biasKV = sb([128, 2], F32, "biasKV")
    uh = sb([128, 4, 514], F32, "uh")
    ghalo = sb([128, NFC, 2], F32, "ghalo")
    kvh = sb([128, 2, 528], BF16, "kvh")
    xt = sb([128, 4, D], F32, "xt")
    hn = sb([128, 4, D], BF16, "hn")
    gbt = sb([128, D], F32, "gbt")
    hT = sb([128, 8, T], BF16, "hT")
    arena = sb([128, NFC * T], BF16, "arena")
    actT = arena[:].rearrange("p (c t) -> p c t", c=NFC)
    Qst = arena[:, 0:4096].rearrange("p (h t) -> p h t", h=8)
    gatesT = arena[0:24, 4096:4096 + T]
    selg = sb([24, 1536], BF16, "selg")
    yaT = sb([128, 4, T], BF16, "yaT")
    ybacc = sb([128, 4, T], F32, "ybacc")
    mixT = ybacc[:].rearrange("p c t -> p (c t)").bitcast(BF16).rearrange("p (c t) -> p c t", c=8)
    ybT = sb([128, 4, T], BF16, "ybT")
    cosT = sb([128, T], F32, "cosT")
    sinT = sb([128, T], F32, "sinT")
    posi = sb([128, T], I32, "posi")
    wr = sb([128, NR, 1024], BF16, "wr")
    NTMP = 8
    tmp = [sb([128, 514], F32, f"tmp{i}") for i in range(NTMP)]
    NPT = 4
    ptr = [sb([128, T], BF16, f"pt{i}") for i in range(NPT)]
    ss = sb([128, 8], F32, "ss")
    smT = sb([128, 2, 4, 128], BF16, "smT")
    impacc = sb([128, 2, 4, 64], F32, "impacc")
    adj = sb([128, 64], F32, "adj")
    wk = sb([128, 64], F32, "wk")
    m8 = sb([128, 16], F32, "m8")
    rden4 = sb([128, 4], F32, "rden4")
    hid = sb([128, 2, 32], BF16, "hid")
    small = sb([128, 4, 32], F32, "small")
    bank = [P.ps([128, 512], F32, f"bank{i}") for i in range(8)]
    BK = [f"B{i}" for i in range(8)]

    ident = cbf[:, C_ID:C_ID + 128]
    perm32 = c32[:, C_PERM:C_PERM + 128]
    ones32 = c32[:, C_ONES:C_ONES + 128]
    CMPM = masks[:, 0:2048]
    TRI_LE = masks[:, 2048:2176]
    TRI_GT = masks[:, 2176:2304]
    ovl = cbf[:, C_OVL:C_OVL + 130].rearrange("p (j c) -> p j c", j=2)

    tmp_i = [0]

    def T_():
        i = tmp_i[0] % NTMP
        tmp_i[0] += 1
        return tmp[i], f"tmp{i}"

    pt_i = [0]

    def PT_():
        i = pt_i[0] % NPT
        pt_i[0] += 1
        return ptr[i], f"pt{i}"

    def mm(out, lhsT, rhs, start, stop, reads, writes):
        P.pe(lambda e: e.matmul(out, lhsT=lhsT, rhs=rhs, start=start, stop=stop), reads, writes)

    P.dma("sp", pp[:], pp_d, writes=["pp"])
    for st in range(4):
        P.dma("sp", xt[:, st, :], x_d[st * 128:(st + 1) * 128, :], writes=[("xt", st)])
    P.dma("sp", gbt[:], gb_d[0], writes=["gbt"])
    P.dma("sp", c32[:], c128_d, writes=["c32"])
    P.dma("pool", cbf[:], c128_d, writes=["cbf"])
    P.dma("pool", selg[:], selg_d, writes=["selg"])
    cast_done = [0]

    def ensure_cast(upto, pace=None):
        upto = min(upto, NCHP // 4)
        while cast_done[0] < upto:
            g4 = cast_done[0]
            P.dma("pool", wsc[g4 * 4:(g4 + 1) * 4].rearrange("c p f -> p c f"), wch_d[g4 * 4:(g4 + 1) * 4].rearrange("c p f -> p c f"),
                  reads=[] if pace is None else [("pace", pace)], writes=[("wsc", g4)])
            cast_done[0] += 1

    P.dve(lambda e: e.memset(uh[:], 0.0), writes=["uh"])
    P.dve(lambda e: e.memset(ghalo[:], 0.0), writes=["ghalo"])
    P.dve(lambda e: e.memset(kvh[:], 0.0), writes=["kvh"])
    P.dve(lambda e: e.memset(kcmpT[:], 0.0), writes=["kcmpT"])
    P.dve(lambda e: e.memset(vcmp[:], 0.0), writes=["vcmp"])
    P.dve(lambda e: e.memset(vcmp[:, :, :, 64:128], 1.0), writes=["vcmp"])
    P.dve(lambda e: e.memset(vsel[:, :, :, 64:128], 1.0), writes=["vsel1"])
    P.dve(lambda e: e.memset(vwin[:, :, :, 64:128], 1.0), writes=["vwin1"])
    P.dve(lambda e: e.memset(smT[:], 0.0), writes=[("smT", 0), ("smT", 1)])
    P.dve(lambda e: e.memset(kwinT[:], 0.0), writes=["kwinT"])
    ensure_cast(6)
    P.dma("pool", masks[:], mask_d[:, 1792:4096], writes=["masks"])
    for i in range(2):
        P.dma("pool", w1bd[:, i, :], cw_d[i], writes=["w1bd"])
    for g in range(2):
        P.dma("pool", kselst[64:128, g, :], exp_d, writes=["kselst_e"])
    wslot = [0]
    paced = set()

    def wload(name):
        s = wslot[0] % NR
        wslot[0] += 1
        idx = CH[name]
        key = ("wr", s)
        g0 = idx // 4
        if cast_done[0] < NCHP // 4 and g0 not in paced:
            paced.add(g0)
            P.dma("sp", wr[:, s, :], wsc[idx], reads=[("wsc", g0)], writes=[key, ("pace", g0)])
            ensure_cast(g0 + 6, pace=g0)
        else:
            P.dma("sp", wr[:, s, :], wsc[idx], reads=[("wsc", g0)], writes=[key])
        return wr[:, s, :], key

    def proj_chunk(name, b):
        w, wkey = wload(name)
        for kc in range(8):
            mm(bank[b][:], w[:, kc * 128:(kc + 1) * 128], hT[:, kc, :], kc == 0, kc == 7, [wkey, "hT"], [BK[b]])

    def rstd_from(src_ap, src_keys, n, scale, dst, dkey):
        np_ = src_ap.shape[0]
        P.act(lambda e: e.activation(out=dst, in_=src_ap, func=AF.Ln, bias=pp[0:np_, PP_EPS:PP_EPS + 1], scale=scale),
              list(src_keys) + ["pp"], [dkey])
        P.act(lambda e: e.activation(out=dst, in_=dst, func=AF.Exp, scale=-0.5), [dkey], [dkey])

    def norm_to_hT(gi):
        gb, gbk = gbt[:], "gbt"
        P.dve(lambda e: e.memset(ss[:, 0:4], 0.0), writes=[("ss", 0), ("ss", 1), ("ss", 2), ("ss", 3)])
        for st in range(4):
            jt_, jk_ = T_()
            junk = jt_[:, 0:512].bitcast(BF16)
            P.act(lambda e, st=st, junk=junk: e.activation(out=junk, in_=xt[:, st, :], func=AF.Square, accum_out=ss[:, st:st + 1]),
                  [("xt", st), ("ss", st)], [("ss", st), jk_])
            rstd_from(ss[:, st:st + 1], [("ss", st)], 1, 1.0 / D, ss[:, 4 + st:5 + st], ("ssr", st))
            P.dve(lambda e, st=st: e.scalar_tensor_tensor(out=hn[:, st, :], in0=xt[:, st, :], scalar=ss[:, 4 + st:5 + st], in1=gb, op0=ALU.mult, op1=ALU.mult),
                  [("xt", st), ("ssr", st), gbk], [("hn", st)])
        for st in range(4):
            b = st % 2
            bb = bank[b][:].bitcast(BF16)
            for kc in range(8):
                P.pe(lambda e, st=st, kc=kc, bb=bb: e.transpose(out=bb[:, kc * 128:(kc + 1) * 128], in_=hn[:, st, kc * 128:(kc + 1) * 128], identity=ident),
                     [("hn", st), "cbf"], [BK[b]])
            P.act(lambda e, st=st, bb=bb: e.activation(out=hT[:, :, st * 128:(st + 1) * 128], in_=bb.rearrange("p (c t) -> p c t", c=8), func=AF.Copy),
                  [BK[b]], ["hT"])
        P.dma("sp", gbt[:], gb_d[1 - gi], writes=["gbt"])

    def rope_phases(name, b, gcol, dsts, dkeys):
        stt = {}

        def A():
            proj_chunk(name, b)
            qraw, qk = T_()
            sq, sk = T_()
            P.act(lambda e: e.activation(out=qraw[:, 0:T], in_=bank[b][:], func=AF.Copy), [BK[b]], [qk])
            P.act(lambda e: e.activation(out=sq[:, 0:T], in_=bank[b][:], func=AF.Square), [BK[b]], [sk])
            stt.update(qraw=qraw, qk=qk, sq=sq, sk=sk)

        def B():
            qraw, qk, sq, sk = stt["qraw"], stt["qk"], stt["sq"], stt["sk"]
            mm(bank[6][:], ones32, sq[:, 0:T], True, True, ["c32", sk], [BK[6]])
            rs, rk = sq, sk
            rstd_from(bank[6][:], [BK[6]], T, 1.0 / 64, rs[:, 0:T], rk)
            P.dve(lambda e: e.scalar_tensor_tensor(out=qraw[:, 0:T], in0=qraw[:, 0:T], scalar=pp[:, gcol:gcol + 1], in1=rs[:, 0:T], op0=ALU.mult, op1=ALU.mult),
                  [qk, rk, "pp"], [qk])

        def C():
            qn, nk, t1, k1 = stt["qraw"], stt["qk"], stt["sq"], stt["sk"]
            mm(bank[7][:], perm32, qn[:, 0:T], True, True, ["c32", nk], [BK[7]])
            P.dve(lambda e: e.tensor_tensor(out=t1[:, 0:T], in0=bank[7][:], in1=sinT[:], op=ALU.mult), [BK[7], "sinT"], [k1])
            P.dve(lambda e: e.tensor_tensor(out=qn[:, 0:T], in0=qn[:, 0:T], in1=cosT[:], op=ALU.mult), [nk, "cosT"], [nk])
            P.dve(lambda e: e.tensor_tensor(out=dsts[0], in0=t1[0:64, 0:T], in1=qn[0:64, 0:T], op=ALU.add), [k1, nk], [dkeys[0]])
            P.dve(lambda e: e.tensor_tensor(out=dsts[1], in0=t1[64:128, 0:T], in1=qn[64:128, 0:T], op=ALU.add), [k1, nk], [dkeys[1]])
        return [A, B, C]

    def combine(h, br, ob, mode):
        row = 3 * h + br
        c, tb = h // 2, 64 * (h % 2)
        rd, rk = T_()
        mm(bank[7][0:64, :], selg[:, row * 64:(row + 1) * 64], gatesT, True, True, ["selg", "gatesT"], [BK[7]])
        bcol = PP_TINY if br == 0 else PP_ZERO
        P.act(lambda e: e.activation(out=rd[0:64, 0:T], in_=bank[ob][64:128, :], func=AF.Ln, bias=pp[0:64, bcol:bcol + 1], scale=1.0), [BK[ob], "pp"], [rk])
        P.act(lambda e: e.activation(out=rd[0:64, 0:T], in_=rd[0:64, 0:T], func=AF.Exp, scale=-1.0), [rk], [rk])
        P.dve(lambda e: e.tensor_tensor(out=rd[0:64, 0:T], in0=rd[0:64, 0:T], in1=bank[7][0:64, :], op=ALU.mult), [rk, BK[7]], [rk])
        pb = 0
        if mode == "first":
            P.dve(lambda e: e.tensor_tensor(out=ybacc[tb:tb + 64, c, :], in0=bank[ob][0:64, :], in1=rd[pb:pb + 64, 0:T], op=ALU.mult),
                  [BK[ob], rk], [("ybacc", h)])
        else:
            t, tk = T_()
            P.dve(lambda e: e.tensor_tensor(out=t[tb:tb + 64, 0:T], in0=bank[ob][0:64, :], in1=rd[pb:pb + 64, 0:T], op=ALU.mult),
                  [BK[ob], rk], [tk])
            if mode == "mid":
                P.dve(lambda e: e.tensor_tensor(out=ybacc[tb:tb + 64, c, :], in0=ybacc[tb:tb + 64, c, :], in1=t[tb:tb + 64, 0:T], op=ALU.add),
                       [tk, ("ybacc", h)], [("ybacc", h)])
            else:
                P.dve(lambda e: e.tensor_tensor(out=ybT[tb:tb + 64, c, :], in0=ybacc[tb:tb + 64, c, :], in1=t[tb:tb + 64, 0:T], op=ALU.add),
                       [tk, ("ybacc", h)], [("ybT", c)])

    def attn_pipeline(items):
        LA = 2
        n = len(items)
        pend = []
        for i in range(n + LA):
            if i < n:
                it = items[i]
                sbk = i % 3
                c0, c1 = it.get("c0", 0), it.get("c1", T)
                has_tri = it.get("tri") is not None
                mm(bank[sbk][:, c0:c1], it["lhsK"], it["rhsQ"][:, c0:c1], True, it["mask"] is None and not has_tri, it["kreads"], [BK[sbk]])
                if it["mask"] is not None:
                    mm(bank[sbk][:, c0:c1], ident, it["mask"], False, True, ["cbf", "masks"], [BK[sbk]])
                if has_tri:
                    tcl = it["tricol"]
                    mm(bank[sbk][:, tcl:tcl + 128], ident, it["tri"], False, True, ["cbf", "masks"], [BK[sbk]])
                pt, pk = PT_()
                P.act(lambda e, pt=pt, sbk=sbk, c0=c0, c1=c1: e.activation(out=pt[:, c0:c1], in_=bank[sbk][:, c0:c1], func=AF.Exp, scale=0.125), [BK[sbk]], [pk])
                pend.append((pt, pk, c0, c1))
            if i >= LA:
                it = items[i - LA]
                pt, pk, c0, c1 = pend[i - LA]
                mm(bank[it["ob"]][:, c0:c1], it["vl"], pt[:, c0:c1], it["first"], it["last"], it["vreads"] + [pk], [BK[it["ob"]]])
                if it.get("extra") is not None:
                    it["extra"](pt, pk)
                if it["post"] is not None:
                    it["post"]()

    out_ops = []
    def rope_tables(tt):
        tok0 = tt * T
        P.dma("sp", posi[:], pos_d[:, tok0:tok0 + T].partition_broadcast(128), writes=["posi"])
        ang, ak = T_()
        P.dve(lambda e: e.tensor_copy(out=ang[:, 0:T], in_=posi[:]), ["posi"], [ak])
        P.dve(lambda e: e.tensor_scalar(out=ang[:, 0:T], in0=ang[:, 0:T], scalar1=pp[:, PP_INV:PP_INV + 1], scalar2=None, op0=ALU.mult), [ak, "pp"], [ak])
        for which in range(2):
            dst, dk = (sinT, "sinT") if which == 0 else (cosT, "cosT")
            a2, a2k = T_()
            shift = 0.0 if which == 0 else math.pi / 2
            P.dve(lambda e, a2=a2, shift=shift: e.tensor_scalar(out=a2[:, 0:T], in0=ang[:, 0:T], scalar1=shift, scalar2=None, op0=ALU.add), [ak], [a2k])
            ki, kik = T_()
            kiv = ki[:, 0:T].bitcast(I32)
            P.dve(lambda e, a2=a2, kiv=kiv: e.tensor_scalar(out=kiv, in0=a2[:, 0:T], scalar1=1.0 / (2 * math.pi), scalar2=None, op0=ALU.mult), [a2k], [kik])
            kf, kfk = T_()
            P.dve(lambda e, kf=kf, kiv=kiv: e.tensor_copy(out=kf[:, 0:T], in_=kiv), [kik], [kfk])
            P.dve(lambda e, kf=kf, a2=a2: e.scalar_tensor_tensor(out=a2[:, 0:T], in0=kf[:, 0:T], scalar=-2 * math.pi, in1=a2[:, 0:T], op0=ALU.mult, op1=ALU.add),
                  [kfk, a2k], [a2k])
            P.dve(lambda e, a2=a2: e.tensor_scalar(out=a2[:, 0:T], in0=a2[:, 0:T], scalar1=3.14159, scalar2=-3.14159, op0=ALU.min, op1=ALU.max), [a2k], [a2k])
            if which == 0:
                P.act(lambda e, a2=a2: e.activation(out=sinT[:], in_=a2[:, 0:T], func=AF.Sin, scale=pp[:, PP_SGN:PP_SGN + 1]), [a2k, "pp"], ["sinT"])
            else:
                P.act(lambda e, a2=a2: e.activation(out=cosT[:], in_=a2[:, 0:T], func=AF.Sin), [a2k], ["cosT"])

    def compute_bias_kv():
        for i in range(2):
            pcol = C_PEK if i == 0 else C_PEV
            for l in range(32):
                mm(bank[7][:, i:i + 1], w1bd[:, i, l * 128:(l + 1) * 128], cbf[:, pcol + l:pcol + l + 1], l == 0, l == 31,
                   ["w1bd", "cbf"], [BK[7]])
        P.act(lambda e: e.activation(out=biasKV[:], in_=bank[7][:, 0:2], func=AF.Copy), [BK[7]], ["biasKV"])

    def do_tile(tt):
        tok0 = tt * T
        if tt > 0:
            for st in range(4):
                P.dma("pool", xt[:, st, :], x_d[tok0 + st * 128:tok0 + (st + 1) * 128, :], writes=[("xt", st)])
        norm_to_hT(0)
        mixst = {}

        def mixerA1(ci):
            proj_chunk(f"ac{ci}", 3)
            tc, tck = T_()
            P.act(lambda e: e.activation(out=tc[:, 0:T], in_=bank[3][:], func=AF.Copy), [BK[3]], [tck])
            proj_chunk(f"ax{ci}", 4)
            P.dve(lambda e: e.tensor_tensor(out=uh[:, ci, 2:514], in0=bank[4][:], in1=tc[:, 0:T], op=ALU.mult), [BK[4], tck], [("uh", ci)])
            cv, cvk = tc, tck
            wc = PP_CA + ci * 3
            P.dve(lambda e: e.tensor_scalar(out=cv[:, 0:T], in0=uh[:, ci, 0:512], scalar1=pp[:, wc:wc + 1], scalar2=None, op0=ALU.mult),
                  [("uh", ci), "pp"], [cvk])
            for k in (1, 2):
                P.dve(lambda e, k=k: e.scalar_tensor_tensor(out=cv[:, 0:T], in0=uh[:, ci, k:k + 512], scalar=pp[:, wc + k:wc + k + 1],
                                                            in1=cv[:, 0:T], op0=ALU.mult, op1=ALU.add), [("uh", ci), "pp", cvk], [cvk])
            P.dve(lambda e: e.tensor_copy(out=uh[:, ci, 0:2], in_=uh[:, ci, 512:514]), [("uh", ci)], [("uh", ci)])
            mixst[ci] = (cv, cvk)

        def mixerA2(ci):
            cv, cvk = mixst[ci]
            proj_chunk(f"ab{ci}", 5)
            P.dve(lambda e: e.tensor_tensor(out=yaT[:, ci, :], in0=bank[5][:], in1=cv[:, 0:T], op=ALU.mult), [BK[5], cvk], [("yaT", ci)])

        def cmp_in(i, nm):
            proj_chunk(nm, 4)
            P.act(lambda e: e.activation(out=kvh[:, i, 16:528], in_=bank[4][:], func=AF.Copy), [BK[4]], [("kvh", i)])

        def vproj(nm, vt, vk, kt0):
            w, wkey = wload(nm)
            for st in range(4):
                for kc in range(8):
                    mm(bank[5][:, st * 128:(st + 1) * 128], hT[:, kc, st * 128:(st + 1) * 128], w[:, kc * 128:(kc + 1) * 128], kc == 0, kc == 7,
                       [wkey, "hT"], [BK[5]])
            P.act(lambda e: e.activation(out=vt[:, kt0:kt0 + 4, :, 0:64], in_=bank[5][:].rearrange("p (s g d) -> p s g d", s=4, g=2),
                                         func=AF.Copy), [BK[5]], [(vk, tt)])

        def gate_all():
            w, wkey = wload("gtT")
            for kc in range(8):
                mm(bank[4][0:24, :], w[:, kc * 128:kc * 128 + 24], hT[:, kc, :], kc == 0, kc == 7, [wkey, "hT"], [BK[4]])
            P.act(lambda e: e.activation(out=gatesT, in_=bank[4][0:24, :], func=AF.Sigmoid), [BK[4]], ["gatesT"])

        wbase = (tt % 2) * 512
        rq = [rope_phases(f"q{c}", c % 3, PP_GQ, [Qst[0:64, 2 * c, :], Qst[0:64, 2 * c + 1, :]], [("Q", 2 * c), ("Q", 2 * c + 1)]) for c in range(4)]
        rks = rope_phases("ksl", 1, PP_GK + 1, [kselst[0:64, 0, tok0:tok0 + T], kselst[0:64, 1, tok0:tok0 + T]], [("ksel", tt), ("ksel", tt)])
        rkw = rope_phases("kwn", 2, PP_GK + 2, [kwinT[0:64, 0, wbase:wbase + T], kwinT[0:64, 1, wbase:wbase + T]], ["kwinT", "kwinT"])
        A_, B_, C_ = 0, 1, 2
        ropes = rq + [rks, rkw]
        f1 = [lambda: mixerA1(0), lambda: mixerA1(1), lambda: mixerA1(2), lambda: mixerA1(3),
              lambda: cmp_in(0, "kc"), lambda: vproj("vsl", vsel, "vsel", 4 * tt)]
        f2 = [lambda: (gate_all(), mixerA2(0)), lambda: mixerA2(1), lambda: mixerA2(2), lambda: mixerA2(3),
              lambda: cmp_in(1, "vc"), lambda: vproj("vwn", vwin, "vwin", (4 * tt) % 8)]
        for k, r in enumerate(ropes):
            r[A_]()
            f1[k]()
            r[B_]()
            f2[k]()
            r[C_]()
        if tt == 0:
            ensure_cast(NCHP // 4, pace=max(paced))
        if tt + 1 < NT:
            rope_tables(tt + 1)
        if tt == 0:
            compute_bias_kv()
        s0 = 32 * tt
        jt_new, po = s0 // 128, s0 % 128
        for i in range(2):
            for l in range(32):
                mm(bank[6][:, i * 32:(i + 1) * 32], w1bd[:, i, l * 128:(l + 1) * 128], kvh[:, i, l:l + 497:16], l == 0, l == 31,
                   ["w1bd", ("kvh", i)], [BK[6]])
            P.act(lambda e, i=i: e.activation(out=hid[:, i, :], in_=bank[6][:, i * 32:(i + 1) * 32], func=AF.Silu, bias=biasKV[:, i:i + 1], scale=1.0),
                  [BK[6], "biasKV"], [("hid", i)])
            P.dve(lambda e, i=i: e.tensor_copy(out=kvh[:, i, 0:16], in_=kvh[:, i, 512:528]), [("kvh", i)], [("kvh", i)])
        mm(bank[7][:, 0:32], cbf[:, C_W2K:C_W2K + 128], hid[:, 0, :], True, True, ["cbf", ("hid", 0)], [BK[7]])
        P.act(lambda e: e.activation(out=small[:, 0, :], in_=bank[7][:, 0:32], func=AF.Copy), [BK[7]], ["small0"])
        P.act(lambda e: e.activation(out=small[:, 1, :], in_=bank[7][:, 0:32], func=AF.Square), [BK[7]], ["small1"])
        mm(bank[7][:, 32:64], ones32, small[:, 1, :], True, True, ["c32", "small1"], [BK[7]])
        rstd_from(bank[7][:, 32:64], [BK[7]], 32, 1.0 / 64, small[:, 2, :], "small2")
        P.dve(lambda e: e.scalar_tensor_tensor(out=small[:, 3, :], in0=small[:, 0, :], scalar=pp[:, PP_GK:PP_GK + 1], in1=small[:, 2, :], op0=ALU.mult, op1=ALU.mult),
              ["small0", "small2", "pp"], ["small3"])
        for g in range(2):
            P.dve(lambda e, g=g: e.tensor_copy(out=kcmpT[0:64, g, s0:s0 + 32], in_=small[g * 64:(g + 1) * 64, 3, :]), ["small3"], ["kcmpT"])
        mm(bank[7][0:32, 128:256], hid[:, 1, :], cbf[:, C_W2V:C_W2V + 128], True, True, ["cbf", ("hid", 1)], [BK[7]])
        P.act(lambda e: e.activation(out=vcmp[po:po + 32, jt_new, :, 0:64], in_=bank[7][0:32, 128:256].rearrange("p (g d) -> p g d", g=2), func=AF.Copy),
              [BK[7]], ["vcmp"])
        if tt == 0:
            P.dve(lambda e: e.memset(vcmp[0:1, 0, :, :], 0.0), writes=["vcmp"])
        jts = [0] if tt < 4 else [0, 1]
        for g in range(2):
            items = []
            for hh in range(4):
                h = 4 * g + hh
                for jt in jts:
                    delta = tt - 4 * jt
                    mask = CMPM[:, 512 * delta:512 * delta + 512] if 0 <= delta <= 3 else None
                    ob = 3 + (h % 2)
                    ib = 5 + (h % 2)

                    def extra(pt, pk, jt=jt, ib=ib):
                        for st in range(4):
                            mm(bank[ib][:, st * 128:st * 128 + 65], pt[:, st * 128:(st + 1) * 128], ovl[:, jt, :], jt == 0, jt == jts[-1], [pk, "cbf"], [BK[ib]])

                    def post(h=h, hh=hh, g=g, ob=ob, ib=ib):
                        combine(h, 0, ob, "first")
                        iv = bank[ib][:].rearrange("p (s c) -> p s c", s=4)
                        P.dve(lambda e: e.tensor_scalar(out=rden4[:], in0=iv[:, :, 64], scalar1=1e-30, scalar2=None, op0=ALU.add), [BK[ib]], ["rden4"])
                        P.dve(lambda e: e.reciprocal(out=rden4[:], in_=rden4[:]), ["rden4"], ["rden4"])
                        for st in range(4):
                            if hh == 0:
                                P.dve(lambda e, st=st: e.tensor_scalar(out=impacc[:, g, st, :], in0=iv[:, st, 0:64], scalar1=rden4[:, st:st + 1], scalar2=None, op0=ALU.mult),
                                      [BK[ib], "rden4"], [("impacc", g)])
                            else:
                                P.dve(lambda e, st=st: e.scalar_tensor_tensor(out=impacc[:, g, st, :], in0=iv[:, st, 0:64], scalar=rden4[:, st:st + 1], in1=impacc[:, g, st, :],
                                                                              op0=ALU.mult, op1=ALU.add), [BK[ib], "rden4", ("impacc", g)], [("impacc", g)])
                    items.append(dict(lhsK=kcmpT[0:64, g, jt * 128:(jt + 1) * 128], rhsQ=Qst[0:64, h, :], kreads=["kcmpT", ("Q", h)], mask=mask,
                                      vl=vcmp[:, jt, g, :], vreads=["vcmp"], ob=ob, first=jt == 0, last=jt == jts[-1], extra=extra,
                                      post=post if jt == jts[-1] else None))
            attn_pipeline(items)
        def sel_chain(g, st):
            c0 = 8 * tt + 2 * st
            lo = 64 - c0
            P.dve(lambda e: e.tensor_tensor(out=adj[:], in0=impacc[:, g, st, :], in1=c32[:, C_WM + lo:C_WM + lo + 64], op=ALU.mult), [("impacc", g), "c32"], ["adj"])
            P.dve(lambda e: e.tensor_tensor(out=adj[:], in0=adj[:], in1=c32[:, C_WB + lo:C_WB + lo + 64], op=ALU.add), ["adj", "c32"], ["adj"])
            P.dve(lambda e: e.memset(adj[:, 0:1], 1e9), ["adj"], ["adj"])
            P.dve(lambda e: e.max(out=m8[:, 0:8], in_=adj[:]), ["adj"], ["m8"])
            P.dve(lambda e: e.match_replace(out=wk[:], in_to_replace=m8[:, 0:8], in_values=adj[:], imm_value=-3e9), ["adj", "m8"], ["wk"])
            P.dve(lambda e: e.max(out=m8[:, 8:16], in_=wk[:]), ["wk", "m8"], ["m8"])
            P.dve(lambda e: e.tensor_scalar(out=smT[:, g, st, 64:128], in0=adj[:], scalar1=m8[:, 15:16], scalar2=1.0, op0=ALU.is_ge, op1=ALU.subtract),
                  ["adj", "m8"], [("smT", g)])

        if tt == 0 and "yc" in dbg:
            P.dma("sp", dbg_d["yc"], ybacc[:, 0, :], reads=[("ybacc", 0), ("ybacc", 1)])
        sel_chain(0, 0)
        sel_chain(0, 1)
        items = []
        for h in range(8):
            g = h // 4
            owb = 3 + (h % 2)
            kts = [kt for kt in range(4 * tt - 4, 4 * tt + 4) if kt >= 0]
            kts = [4 * tt] + [kt for kt in kts if kt != 4 * tt]
            for kt in kts:
                d = kt - 4 * tt
                slot = kt % 8
                if d >= 0:
                    c0, c1, tri, tcl = 128 * d, T, TRI_LE, 128 * d
                else:
                    dd = d + 4
                    c0, c1, tri, tcl = 0, 128 * (dd + 1), TRI_GT, 128 * dd
                items.append(dict(lhsK=kwinT[0:64, g, slot * 128:(slot + 1) * 128], rhsQ=Qst[0:64, h, :], kreads=["kwinT", ("Q", h)], mask=None,
                                  c0=c0, c1=c1, tri=tri, tricol=tcl,
                                  vl=vwin[:, slot, g, :], vreads=[("vwin", kt // 4), "vwin1"], ob=owb, first=kt == kts[0], last=kt == kts[-1],
                                  post=(lambda h=h, owb=owb: (combine(h, 2, owb, "mid"), sel_chain((h + 2) // 4, (h + 2) % 4) if h + 2 < 8 else None)) if kt == kts[-1] else None))
        attn_pipeline(items)
        for g in range(2):
            sbk = 7 - g
            bb = bank[sbk][:].bitcast(BF16)
            for st in range(4):
                P.pe(lambda e, g=g, st=st, bb=bb: e.transpose(out=bb[:, st * 128:(st + 1) * 128], in_=smT[:, g, st, :], identity=ident), [("smT", g), "cbf"], [BK[sbk]])
            for hh in range(4):
                h = 4 * g + hh
                P.act(lambda e, h=h, bb=bb: e.activation(out=Qst[64:128, h, :], in_=bb[64:128, 0:512], func=AF.Copy, scale=-NEGB), [BK[sbk]], [("Qs", h)])
        items = []
        for h in range(8):
            g = h // 4
            osb = 5 + (h % 2)
            nkt = 4 * tt + 4
            order = ([4 * tt] + [kt for kt in range(nkt) if kt != 4 * tt]) if tt == 0 else list(range(nkt))
            for kt in order:
                d = kt - 4 * tt
                if d >= 0:
                    c0, c1, tri, tcl = 128 * d, T, TRI_LE, 128 * d
                else:
                    c0, c1, tri, tcl = 0, T, None, 0
                items.append(dict(lhsK=kselst[:, g, kt * 128:(kt + 1) * 128], rhsQ=Qst[:, h, :], kreads=[("ksel", kt // 4), "kselst_e", ("Q", h), ("Qs", h)], mask=None,
                                  c0=c0, c1=c1, tri=tri, tricol=tcl,
                                  vl=vsel[:, kt, g, :], vreads=[("vsel", kt // 4), "vsel1"], ob=osb, first=kt == order[0], last=kt == order[-1],
                                  post=(lambda h=h, osb=osb: combine(h, 1, osb, "last")) if kt == order[-1] else None))
        attn_pipeline(items)
        if tt == 0 and "ys" in dbg:
            P.dma("sp", dbg_d["ys"], ybacc[:, 0, :], reads=[("ybacc", 0), ("ybacc", 1)])
        for dc in range(8):
            w, wkey = wload(f"wab{dc}")
            o4 = 4 * (dc % 2)
            for kc in range(4):
                mm(bank[o4][:], w[:, kc * 128:(kc + 1) * 128], yaT[:, kc, :], kc == 0, kc == 3, [wkey, ("yaT", kc)], [BK[o4]])
            for kc in range(4):
                mm(bank[o4 + 1][:], w[:, (4 + kc) * 128:(5 + kc) * 128], ybT[:, kc, :], kc == 0, kc == 3, [wkey, ("ybT", kc)], [BK[o4 + 1]])
            proj_chunk(f"gmA{dc}", o4 + 2)
            proj_chunk(f"gmB{dc}", o4 + 3)
            sA, sAk = T_()
            sB, sBk = T_()
            P.act(lambda e, sA=sA, o4=o4: e.activation(out=sA[:, 0:T], in_=bank[o4 + 2][:], func=AF.Sigmoid), [BK[o4 + 2]], [sAk])
            P.act(lambda e, sB=sB, o4=o4: e.activation(out=sB[:, 0:T], in_=bank[o4 + 3][:], func=AF.Sigmoid), [BK[o4 + 3]], [sBk])
            P.dve(lambda e, sA=sA, o4=o4: e.tensor_tensor(out=sA[:, 0:T], in0=bank[o4][:], in1=sA[:, 0:T], op=ALU.mult), [BK[o4], sAk], [sAk])
            P.dve(lambda e, sB=sB, o4=o4: e.tensor_tensor(out=sB[:, 0:T], in0=bank[o4 + 1][:], in1=sB[:, 0:T], op=ALU.mult), [BK[o4 + 1], sBk], [sBk])
            P.dve(lambda e, sA=sA, sB=sB, dc=dc: e.tensor_tensor(out=mixT[:, dc, :], in0=sA[:, 0:T], in1=sB[:, 0:T], op=ALU.add), [sAk, sBk], ["mixT"])
        gi = 0
        for half in range(2):
            chunks = [wload(f"wo{half}_{kq}") for kq in range(4)]
            for st in range(4):
                b = gi % 4
                gi += 1
                for kc in range(8):
                    w, wkey = chunks[kc // 2]
                    mm(bank[b][:], mixT[:, kc, st * 128:(st + 1) * 128], w[:, (kc % 2) * 512:(kc % 2) * 512 + 512], kc == 0, kc == 7, [wkey, "mixT"], [BK[b]])
                P.dve(lambda e, st=st, half=half, b=b: e.tensor_tensor(out=xt[:, st, half * 512:(half + 1) * 512], in0=bank[b][:], in1=xt[:, st, half * 512:(half + 1) * 512], op=ALU.add),
                      [BK[b], ("xt", st)], [("xt", st)])
        norm_to_hT(1)
        for fc in range(NFC):
            gb = 2 * (fc % 4)
            ub = gb + 1
            proj_chunk(f"fg{fc}", gb)
            proj_chunk(f"fu{fc}", ub)
            gs, gk = T_()
            P.act(lambda e, gs=gs, gb=gb: e.activation(out=gs[:, 2:514], in_=bank[gb][:], func=AF.Copy), [BK[gb]], [gk])
            P.dve(lambda e, gs=gs, fc=fc: e.tensor_copy(out=gs[:, 0:2], in_=ghalo[:, fc, :]), [("ghalo", fc), gk], [gk])
            cv, cvk = T_()
            wc = PP_FW + fc * 3
            P.dve(lambda e, cv=cv, gs=gs, wc=wc: e.tensor_scalar(out=cv[:, 0:T], in0=gs[:, 0:512], scalar1=pp[:, wc:wc + 1], scalar2=None, op0=ALU.mult), [gk, "pp"], [cvk])
            for k in (1, 2):
                P.dve(lambda e, cv=cv, gs=gs, wc=wc, k=k: e.scalar_tensor_tensor(out=cv[:, 0:T], in0=gs[:, k:k + 512], scalar=pp[:, wc + k:wc + k + 1], in1=cv[:, 0:T],
                                                                                   op0=ALU.mult, op1=ALU.add), [gk, "pp", cvk], [cvk])
            P.dve(lambda e, gs=gs, fc=fc: e.tensor_copy(out=ghalo[:, fc, :], in_=gs[:, 512:514]), [gk], [("ghalo", fc)])
            P.act(lambda e, cv=cv, fc=fc: e.activation(out=cv[:, 0:T], in_=cv[:, 0:T], func=AF.Silu, bias=pp[:, PP_FB + fc:PP_FB + fc + 1], scale=1.0), [cvk, "pp"], [cvk])
            if fc >= NFC - 4:
                us, usk = T_()
                P.act(lambda e, us=us, ub=ub: e.activation(out=us[:, 0:T], in_=bank[ub][:], func=AF.Copy), [BK[ub]], [usk])
                P.dve(lambda e, cv=cv, fc=fc, us=us: e.tensor_tensor(out=actT[:, fc, :], in0=us[:, 0:T], in1=cv[:, 0:T], op=ALU.mult), [usk, cvk], [("actT", fc)])
            else:
                P.dve(lambda e, cv=cv, fc=fc, ub=ub: e.tensor_tensor(out=actT[:, fc, :], in0=bank[ub][:], in1=cv[:, 0:T], op=ALU.mult), [BK[ub], cvk], [("actT", fc)])
        for fc in range(NFC):
            w, wkey = wload(f"wd{fc}")
            for st in ((2, 3, 0, 1) if fc == 0 else (0, 1, 2, 3)):
                for half in range(2):
                    b = 2 * st + half
                    mm(bank[b][:], actT[:, fc, st * 128:(st + 1) * 128], w[:, half * 512:(half + 1) * 512], fc == 0, fc == NFC - 1, [wkey, ("actT", fc)], [BK[b]])
        for st in range(4):
            for half in range(2):
                b = 2 * st + half
                P.dve(lambda e, st=st, half=half, b=b: e.tensor_tensor(out=xt[:, st, half * 512:(half + 1) * 512], in0=bank[b][:], in1=xt[:, st, half * 512:(half + 1) * 512], op=ALU.add),
                      [BK[b], ("xt", st)], [("xt", st)])
            out_ops.append(P.dma("pool", out_d[tok0 + st * 128:tok0 + (st + 1) * 128, :], xt[:, st, :], reads=[("xt", st)]))
    rope_tables(0)
    for tt_ in range(NT):
        do_tile(tt_)
    P.emit(final_wait_ops=[o for o in P.ops if o.dma])
    P.close()
    return nc, P.stats


_CACHE = {}


def kernel(**inputs):
    lay = host_layout(inputs)
    if "nc" not in _CACHE:
        _CACHE["nc"] = build()[0]
    nc = _CACHE["nc"]
    x = np.asarray(inputs["x"], np.float32)
    pos = np.asarray(inputs["positions"], np.int32)
    in_maps = []
    for b in range(8):
        m = dict(lay)
        m["x"] = np.ascontiguousarray(x[b])
        m["pos"] = np.ascontiguousarray(pos[b:b + 1])
        in_maps.append(m)
    res = run_bass_kernel_spmd(nc, in_maps, core_ids=list(range(8)))
    return np.stack([np.asarray(r["out"], np.float32) for r in res.results], axis=0)
```

```python
from contextlib import ExitStack
import math
import numpy as np
import concourse.bass as bass
import concourse.mybir as mybir
from concourse.bass_utils import run_bass_kernel_spmd

F32 = mybir.dt.float32
BF16 = mybir.dt.bfloat16
I32 = mybir.dt.int32
AF = mybir.ActivationFunctionType
ALU = mybir.AluOpType

EPOCH = 12000
N_DMA_SEMS = 8
S = 4096
D = 1024
T = 512
NTILES = S // T
DFF = 2816
NFC = DFF // 128
EPS = 1e-6
NEGB = -30000.0
NR = 8
import os
SYNC_SAME = os.environ.get("KSYNC", "1") == "1"


class Op:
    __slots__ = ("eng", "fn", "raw", "oth", "dma", "need_inc", "sem", "val", "idx")

    def __init__(self, eng, fn, dma):
        self.eng = eng
        self.fn = fn
        self.dma = dma
        self.raw = set()
        self.oth = set()
        self.need_inc = False
        self.sem = None
        self.val = 0


class Prog:
    COMPUTE = ("pe", "act", "dve", "pool")
    ALL = ("pe", "act", "dve", "pool", "sp")

    def __init__(self, nc):
        self.nc = nc
        self.ops = []
        self.last_w = {}
        self.readers = {}
        self.stack = ExitStack()
        self.n_sb = 0

    def sb(self, shape, dtype, name=None):
        self.n_sb += 1
        name = (name or f"sb{self.n_sb}") + "_sb"
        return self.stack.enter_context(self.nc.sbuf_tensor(name, list(shape), dtype))

    def ps(self, shape, dtype, name=None):
        self.n_sb += 1
        name = name or f"ps{self.n_sb}"
        return self.stack.enter_context(self.nc.psum_tensor(name, list(shape), dtype))

    def add(self, eng, fn, reads=(), writes=(), dma=False):
        op = Op(eng, fn, dma)
        for k in reads:
            w = self.last_w.get(k)
            if w is not None:
                op.raw.add(w)
        for k in writes:
            w = self.last_w.get(k)
            if w is not None:
                op.oth.add(w)
            for r in self.readers.get(k, ()):
                op.oth.add(r)
        for k in writes:
            self.last_w[k] = op
            self.readers[k] = []
        for k in reads:
            if k not in writes:
                self.readers.setdefault(k, []).append(op)
        op.idx = len(self.ops)
        self.ops.append(op)
        return op

    def pe(self, fn, reads=(), writes=()):
        return self.add("pe", fn, reads, writes)

    def act(self, fn, reads=(), writes=()):
        return self.add("act", fn, reads, writes)

    def dve(self, fn, reads=(), writes=()):
        return self.add("dve", fn, reads, writes)

    def pool(self, fn, reads=(), writes=()):
        return self.add("pool", fn, reads, writes)

    def dma(self, eng, out, in_, reads=(), writes=()):
        return self.add(eng, lambda e: e.dma_start(out=out, in_=in_), reads, writes, dma=True)

    def emit(self, final_wait_ops=()):
        nc = self.nc
        ops = self.ops
        deps_of = []
        dma_count = {e: 0 for e in self.ALL}
        dma_prev = {}
        for op in ops:
            deps = set()
            for d in op.raw:
                if d.eng == op.eng and not d.dma and not op.dma and op.eng == "pe":
                    continue
                deps.add(d)
            for d in op.oth:
                if d.eng == op.eng and not d.dma and not op.dma and (op.eng == "pe" or not SYNC_SAME):
                    continue
                deps.add(d)
            if op.dma:
                slot = dma_count[op.eng] % N_DMA_SEMS
                dma_count[op.eng] += 1
                p = dma_prev.get((op.eng, slot))
                if p is not None:
                    deps.add(p)
                dma_prev[(op.eng, slot)] = op
                op.sem = ("dma", op.eng, slot)
                op.need_inc = True
            for d in deps:
                d.need_inc = True
            deps_of.append(deps)
        self.add("sp", None)
        fdeps = set(final_wait_ops)
        for d in fdeps:
            d.need_inc = True
        deps_of.append(fdeps)
        cnt = {e: 0 for e in self.COMPUTE}
        dcnt = {}
        sem_names = set()
        for op in ops:
            if not op.need_inc:
                continue
            if op.dma:
                dcnt[op.sem] = dcnt.get(op.sem, 0) + 16
                op.val = dcnt[op.sem]
            else:
                c = cnt[op.eng]
                op.sem = ("c", op.eng, c // EPOCH)
                op.val = c % EPOCH + 1
                cnt[op.eng] = c + 1
            sem_names.add(op.sem)
        sems = {}
        for s in sorted(sem_names):
            sems[s] = self.stack.enter_context(nc.semaphore("s_" + "_".join(map(str, s))))
        per_eng = {e: [] for e in self.ALL}
        for op, deps in zip(ops, deps_of):
            per_eng[op.eng].append((op, deps))
        n_wait = [0]

        def run(eng_name, e):
            known = {}
            for op, deps in per_eng[eng_name]:
                need = {}
                for d in deps:
                    if d.val > need.get(d.sem, 0):
                        need[d.sem] = d.val
                for s, v in need.items():
                    if known.get(s, 0) >= v:
                        continue
                    e.wait_ge(sems[s], v)
                    n_wait[0] += 1
                    known[s] = v
                if op.fn is None:
                    continue
                ins = op.fn(e)
                if op.need_inc:
                    ins.then_inc(sems[op.sem], 16 if op.dma else 1)

        with nc.Block() as block:
            @block.tensor
            def _(e):
                run("pe", e)

            @block.scalar
            def _(e):
                run("act", e)

            @block.vector
            def _(e):
                run("dve", e)

            @block.gpsimd
            def _(e):
                run("pool", e)

            @block.sync
            def _(e):
                run("sp", e)
        self.stats = dict(n_ops=len(ops), n_inc=sum(1 for o in ops if o.need_inc), n_sems=len(sems), n_wait=n_wait[0],
                          per_eng={k: len(v) for k, v in per_eng.items()})

    def close(self):
        self.stack.close()


SPLIT = [512, 512, 512, 512] + [128] * 6 + [24, 2048]
OFF = np.concatenate([[0], np.cumsum(SPLIT)]).tolist()
O_AB, O_AC, O_AX, O_Q, O_KC, O_VC, O_KSL, O_VSL, O_KWN, O_VWN, O_GN, O_GM = OFF[:12]

CH_NAMES = []
for ci in range(4):
    CH_NAMES += [f"ac{ci}", f"ax{ci}", f"ab{ci}"]
CH_NAMES += [f"q{c}" for c in range(4)] + ["ksl", "kwn", "kc", "vc", "vsl", "vwn"]
CH_NAMES += ["gtT"]
for dc in range(8):
    CH_NAMES += [f"wab{dc}", f"gmA{dc}", f"gmB{dc}"]
CH_NAMES += [f"wo{h}_{kq}" for h in range(2) for kq in range(4)]
for fc in range(NFC):
    CH_NAMES += [f"fg{fc}", f"fu{fc}"]
CH_NAMES += [f"wd{fc}" for fc in range(NFC)]
CH = {n: i for i, n in enumerate(CH_NAMES)}
NCH = len(CH_NAMES)
NCHP = (NCH + 3) // 4 * 4

PP_CA = 0
PP_GQ = 12
PP_GK = 13
PP_FW = 16
PP_FB = 82
PP_INV = 104
PP_SGN = 105
PP_G1 = 106
PP_G2 = 114
PP_EPS = 122
PP_TINY = 123
PP_ZERO = 124
NPP = 126

C_ID, C_PERM, C_ONES, C_W2K, C_W2V, C_WM, C_WB, C_PEK, C_PEV, C_OVL = 0, 128, 256, 384, 512, 640, 768, 896, 928, 960
NC128 = 960 + 130


def _chunk_cols(w, cols):
    sub = w[:, cols]
    return sub.reshape(8, 128, len(cols)).transpose(1, 0, 2).reshape(128, 8 * len(cols))


def host_layout(inp):
    L = 0
    w_in = np.asarray(inp["w_in"][L], np.float32)
    chunks = np.zeros((NCHP, 128, 1024), np.float32)

    def put(name, arr):
        chunks[CH[name]] = arr

    ar = np.arange
    for ci in range(4):
        put(f"ac{ci}", _chunk_cols(w_in, O_AC + ci * 128 + ar(128)))
        put(f"ax{ci}", _chunk_cols(w_in, O_AX + ci * 128 + ar(128)))
        put(f"ab{ci}", _chunk_cols(w_in, O_AB + ci * 128 + ar(128)))
        put(f"q{ci}", _chunk_cols(w_in, O_Q + ci * 128 + ar(128)))
    for n, o in (("ksl", O_KSL), ("kwn", O_KWN), ("kc", O_KC), ("vc", O_VC), ("vsl", O_VSL), ("vwn", O_VWN)):
        put(n, _chunk_cols(w_in, o + ar(128)))
    gt = np.zeros((128, 8, 128), np.float32)
    gt[:, :, 0:24] = _chunk_cols(w_in, O_GN + ar(24)).reshape(128, 8, 24)
    put("gtT", gt.reshape(128, 1024))
    w_a = np.asarray(inp["w_a_out"][L], np.float32)
    w_b = np.asarray(inp["w_b_out"][L], np.float32)
    for dc in range(8):
        a = w_a[:, dc * 128:(dc + 1) * 128].reshape(4, 128, 128).transpose(1, 0, 2)
        b = w_b[:, dc * 128:(dc + 1) * 128].reshape(4, 128, 128).transpose(1, 0, 2)
        put(f"wab{dc}", np.concatenate([a, b], axis=1).reshape(128, 1024))
        put(f"gmA{dc}", _chunk_cols(w_in, O_GM + dc * 128 + ar(128)))
        put(f"gmB{dc}", _chunk_cols(w_in, O_GM + 1024 + dc * 128 + ar(128)))
    w_o = np.asarray(inp["w_o"][L], np.float32)
    for h in range(2):
        for kq in range(4):
            blk = w_o[kq * 256:(kq + 1) * 256, h * 512:(h + 1) * 512]
            put(f"wo{h}_{kq}", blk.reshape(2, 128, 512).transpose(1, 0, 2).reshape(128, 1024))
    w_f = np.asarray(inp["w_ffn_in"][L], np.float32)
    w_d = np.asarray(inp["w_ffn_out"][L], np.float32)
    for fc in range(NFC):
        put(f"fg{fc}", _chunk_cols(w_f, fc * 128 + ar(128)))
        put(f"fu{fc}", _chunk_cols(w_f, DFF + fc * 128 + ar(128)))
        put(f"wd{fc}", w_d[fc * 128:(fc + 1) * 128, :])
    pp = np.zeros((128, NPP), np.float32)
    ca = np.asarray(inp["conv_a_w"][L], np.float32)
    pp[:, PP_CA:PP_CA + 12] = ca.reshape(3, 4, 128).transpose(2, 1, 0).reshape(128, 12)
    pp[:, PP_GQ] = np.tile(np.asarray(inp["q_norm_g"][L], np.float32), 2)
    kg = np.asarray(inp["k_norm_g"][L], np.float32)
    for i in range(3):
        pp[:, PP_GK + i] = np.tile(kg[i], 2)
    fw = np.asarray(inp["ffn_conv_w"][L], np.float32)
    pp[:, PP_FW:PP_FW + 66] = fw.reshape(3, NFC, 128).transpose(2, 1, 0).reshape(128, 66)
    pp[:, PP_FB:PP_FB + 22] = np.asarray(inp["ffn_conv_b"][L], np.float32).reshape(NFC, 128).T
    inv = (10000.0 ** (-np.arange(32, dtype=np.float32) / 32)).astype(np.float32)
    pp[:, PP_INV] = np.tile(inv, 4)
    pp[:, PP_SGN] = np.tile(np.concatenate([-np.ones(32), np.ones(32)]), 2)
    pp[:, PP_G1:PP_G1 + 8] = np.asarray(inp["norm1_g"][L], np.float32).reshape(8, 128).T
    pp[:, PP_G2:PP_G2 + 8] = np.asarray(inp["norm2_g"][L], np.float32).reshape(8, 128).T
    pp[:, PP_EPS] = EPS
    pp[:, PP_TINY] = 1e-20
    pp[:, PP_ZERO] = 0.0
    cw = np.zeros((2, 128, 32, 128), np.float32)
    for i, nm in enumerate(("cmp_k_w1", "cmp_v_w1")):
        w1 = np.asarray(inp[nm][L], np.float32).reshape(32, 64, 64)
        cw[i, 0:64, :, 0:64] = w1.transpose(1, 0, 2)
        cw[i, 64:128, :, 64:128] = w1.transpose(1, 0, 2)
    c128 = np.zeros((128, NC128), np.float32)
    c128[:, C_ID:C_ID + 128] = np.eye(128)
    perm = np.zeros((128, 128), np.float32)
    for m in range(128):
        k = (m // 64) * 64 + ((m % 64) + 32) % 64
        perm[k, m] = 1.0
    c128[:, C_PERM:C_PERM + 128] = perm
    ones = np.zeros((128, 128), np.float32)
    ones[0:64, 0:64] = 1.0
    ones[64:128, 64:128] = 1.0
    c128[:, C_ONES:C_ONES + 128] = ones
    for c0, nm in ((C_W2K, "cmp_k_w2"), (C_W2V, "cmp_v_w2")):
        w2 = np.asarray(inp[nm][L], np.float32)
        c128[0:64, c0:c0 + 64] = w2
        c128[64:128, c0 + 64:c0 + 128] = w2
    hi = (np.arange(128) >= 64).astype(np.int64)[:, None]
    kk = np.arange(128)[None, :] - 64
    c128[:, C_WM:C_WM + 128] = (kk <= hi - 2)
    wb = np.zeros((128, 128), np.float32)
    wb[(kk == hi) | (kk == hi - 1)] = 1e9
    wb[kk > hi] = -1e9
    c128[:, C_WB:C_WB + 128] = wb
    c128[:, C_PEK:C_PEK + 32] = np.tile(np.asarray(inp["cmp_k_pe"][L], np.float32).T, (2, 1))
    c128[:, C_PEV:C_PEV + 32] = np.tile(np.asarray(inp["cmp_v_pe"][L], np.float32).T, (2, 1))
    ovl = np.zeros((2, 128, 65), np.float32)
    for s in range(1, 256):
        i = s - 1
        for j in range(64):
            if 16 * i < 64 * j + 64 and 16 * i + 32 > 64 * j:
                ovl[s // 128, s % 128, j] = 1.0
        ovl[s // 128, s % 128, 64] = 1.0
    c128[:, C_OVL:C_OVL + 130] = ovl.transpose(1, 0, 2).reshape(128, 130)
    p = np.arange(128)[:, None]
    k = np.arange(896)[None, :]
    caus = np.where(p <= k - 384, 0.0, NEGB).astype(np.float32)
    winlo = np.where(p > k - 384, 0.0, NEGB).astype(np.float32)
    k2 = np.arange(2048)[None, :]
    cmpm = np.where(k2 >= 16 * p + 15, 0.0, NEGB).astype(np.float32)
    pj = np.arange(128)[None, :]
    tri_le = np.where(p <= pj, 0.0, NEGB).astype(np.float32)
    tri_gt = np.where(p > pj, 0.0, NEGB).astype(np.float32)
    masks = np.concatenate([caus, winlo, cmpm, tri_le, tri_gt], axis=1)
    expd = (np.arange(4096)[None, :] // 64 == np.arange(64)[:, None]).astype(np.float32)
    gb = np.stack([np.broadcast_to(np.asarray(inp["norm1_g"][L], np.float32)[None, :], (128, D)),
                   np.broadcast_to(np.asarray(inp["norm2_g"][L], np.float32)[None, :], (128, D))]).copy()
    selg = np.zeros((24, 24, 64), np.float32)
    for r in range(24):
        selg[r, r, :] = 1.0
    return dict(selg=selg.reshape(24, 1536), gb=gb, wch=chunks, pp=pp, cw=cw.reshape(2, 128, 4096), c128=c128, masks=masks, expd=expd)


def build(NT=NTILES, dbg=()):
    nc = bass.Bass("TRN2", target_bir_lowering=False)
    x_d = nc.dram_tensor("x", [S, D], F32, kind="ExternalInput").ap()
    pos_d = nc.dram_tensor("pos", [1, S], I32, kind="ExternalInput").ap()
    wch_d = nc.dram_tensor("wch", [NCHP, 128, 1024], F32, kind="ExternalInput").ap()
    pp_d = nc.dram_tensor("pp", [128, NPP], F32, kind="ExternalInput").ap()
    cw_d = nc.dram_tensor("cw", [2, 128, 4096], F32, kind="ExternalInput").ap()
    c128_d = nc.dram_tensor("c128", [128, NC128], F32, kind="ExternalInput").ap()
    mask_d = nc.dram_tensor("masks", [128, 4096], F32, kind="ExternalInput").ap()
    exp_d = nc.dram_tensor("expd", [64, 4096], F32, kind="ExternalInput").ap()
    gb_d = nc.dram_tensor("gb", [2, 128, D], F32, kind="ExternalInput").ap()
    selg_d = nc.dram_tensor("selg", [24, 1536], F32, kind="ExternalInput").ap()
    out_d = nc.dram_tensor("out", [S, D], F32, kind="ExternalOutput").ap()
    wsc = nc.dram_tensor("wsc", [NCHP, 128, 1024], BF16, kind="Internal").ap()
    dbg_d = {n: nc.dram_tensor("dbg_" + n, [128, 512], F32, kind="ExternalOutput").ap() for n in dbg}

    P = Prog(nc)
    sb = P.sb
    pp = sb([128, NPP], F32, "pp")
    c32 = sb([128, NC128], F32, "c32")
    cbf = sb([128, NC128], BF16, "cbf")
    masks = sb([128, 2048 + 256], BF16, "masksb")
    w1bd = sb([128, 2, 4096], BF16, "w1bd")
    kselst = sb([128, 2, S], BF16, "kselst")
    vsel = sb([128, 32, 2, 128], BF16, "vsel")
    kwinT = sb([128, 2, 1024], BF16, "kwinT")
    vwin = sb([128, 8, 2, 128], BF16, "vwin")
    kcmpT = sb([128, 2, 256], BF16, "kcmpT")
    vcmp = sb([128, 2, 2, 128], BF16, "vcmp")
    biasKV = sb([128, 2], F32, "biasKV")
    uh = sb([128, 4, 514], F32, "uh")
    ghalo = sb([128, NFC, 2], F32, "ghalo")
    kvh = sb([128, 2, 528], BF16, "kvh")
    xt = sb([128, 4, D], F32, "xt")
    hn = sb([128, 4, D], BF16, "hn")
    gbt = sb([128, D], F32, "gbt")
    hT = sb([128, 8, T], BF16, "hT")
    arena = sb([128, NFC * T], BF16, "arena")
    actT = arena[:].rearrange("p (c t) -> p c t", c=NFC)
    Qst = arena[:, 0:4096].rearrange("p (h t) -> p h t", h=8)
    gatesT = arena[0:24, 4096:4096 + T]
    selg = sb([24, 1536], BF16, "selg")
    yaT = sb([128, 4, T], BF16, "yaT")
    ybacc = sb([128, 4, T], F32, "ybacc")
    mixT = ybacc[:].rearrange("p c t -> p (c t)").bitcast(BF16).rearrange("p (c t) -> p c t", c=8)
    ybT = sb([128, 4, T], BF16, "ybT")
    cosT = sb([128, T], F32, "cosT")
    sinT = sb([128, T], F32, "sinT")
    posi = sb([128, T], I32, "posi")
    wr = sb([128, NR, 1024], BF16, "wr")
    NTMP = 8
    tmp = [sb([128, 514], F32, f"tmp{i}") for i in range(NTMP)]
    NPT = 4
    ptr = [sb([128, T], BF16, f"pt{i}") for i in range(NPT)]
    ss = sb([128, 8], F32, "ss")
    smT = sb([128, 2, 4, 128], BF16, "smT")
    impacc = sb([128, 2, 4, 64], F32, "impacc")
    adj = sb([128, 64], F32, "adj")
    wk = sb([128, 64], F32, "wk")
    m8 = sb([128, 16], F32, "m8")
    rden4 = sb([128, 4], F32, "rden4")
    hid = sb([128, 2, 32], BF16, "hid")
    small = sb([128, 4, 32], F32, "small")
    bank = [P.ps([128, 512], F32, f"bank{i}") for i in range(8)]
    BK = [f"B{i}" for i in range(8)]

    ident = cbf[:, C_ID:C_ID + 128]
    perm32 = c32[:, C_PERM:C_PERM + 128]
    ones32 = c32[:, C_ONES:C_ONES + 128]
    CMPM = masks[:, 0:2048]
    TRI_LE = masks[:, 2048:2176]
    TRI_GT = masks[:, 2176:2304]
    ovl = cbf[:, C_OVL:C_OVL + 130].rearrange("p (j c) -> p j c", j=2)

    tmp_i = [0]

    def T_():
        i = tmp_i[0] % NTMP
        tmp_i[0] += 1
        return tmp[i], f"tmp{i}"

    pt_i = [0]

    def PT_():
        i = pt_i[0] % NPT
        pt_i[0] += 1
        return ptr[i], f"pt{i}"

    def mm(out, lhsT, rhs, start, stop, reads, writes):
        P.pe(lambda e: e.matmul(out, lhsT=lhsT, rhs=rhs, start=start, stop=stop), reads, writes)

    P.dma("sp", pp[:], pp_d, writes=["pp"])
    for st in range(4):
        P.dma("sp", xt[:, st, :], x_d[st * 128:(st + 1) * 128, :], writes=[("xt", st)])
    P.dma("sp", gbt[:], gb_d[0], writes=["gbt"])
    P.dma("sp", c32[:], c128_d, writes=["c32"])
    P.dma("pool", cbf[:], c128_d, writes=["cbf"])
    P.dma("pool", selg[:], selg_d, writes=["selg"])
    cast_done = [0]

    def ensure_cast(upto, pace=None):
        upto = min(upto, NCHP // 4)
        while cast_done[0] < upto:
            g4 = cast_done[0]
            P.dma("pool", wsc[g4 * 4:(g4 + 1) * 4].rearrange("c p f -> p c f"), wch_d[g4 * 4:(g4 + 1) * 4].rearrange("c p f -> p c f"),
                  reads=[] if pace is None else [("pace", pace)], writes=[("wsc", g4)])
            cast_done[0] += 1

    P.dve(lambda e: e.memset(uh[:], 0.0), writes=["uh"])
    P.dve(lambda e: e.memset(ghalo[:], 0.0), writes=["ghalo"])
    P.dve(lambda e: e.memset(kvh[:], 0.0), writes=["kvh"])
    P.dve(lambda e: e.memset(kcmpT[:], 0.0), writes=["kcmpT"])
    P.dve(lambda e: e.memset(vcmp[:], 0.0), writes=["vcmp"])
    P.dve(lambda e: e.memset(vcmp[:, :, :, 64:128], 1.0), writes=["vcmp"])
    P.dve(lambda e: e.memset(vsel[:, :, :, 64:128], 1.0), writes=["vsel1"])
    P.dve(lambda e: e.memset(vwin[:, :, :, 64:128], 1.0), writes=["vwin1"])
    P.dve(lambda e: e.memset(smT[:], 0.0), writes=[("smT", 0), ("smT", 1)])
    P.dve(lambda e: e.memset(kwinT[:], 0.0), writes=["kwinT"])
    ensure_cast(6)
    P.dma("pool", masks[:], mask_d[:, 1792:4096], writes=["masks"])
    for i in range(2):
        P.dma("pool", w1bd[:, i, :], cw_d[i], writes=["w1bd"])
    for g in range(2):
        P.dma("pool", kselst[64:128, g, :], exp_d, writes=["kselst_e"])
    wslot = [0]
    paced = set()

    def wload(name):
        s = wslot[0] % NR
        wslot[0] += 1
        idx = CH[name]
        key = ("wr", s)
        g0 = idx // 4
        if cast_done[0] < NCHP // 4 and g0 not in paced:
            paced.add(g0)
            P.dma("sp", wr[:, s, :], wsc[idx], reads=[("wsc", g0)], writes=[key, ("pace", g0)])
            ensure_cast(g0 + 6, pace=g0)
        else:
            P.dma("sp", wr[:, s, :], wsc[idx], reads=[("wsc", g0)], writes=[key])
        return wr[:, s, :], key

    def proj_chunk(name, b):
        w, wkey = wload(name)
        for kc in range(8):
            mm(bank[b][:], w[:, kc * 128:(kc + 1) * 128], hT[:, kc, :], kc == 0, kc == 7, [wkey, "hT"], [BK[b]])

    def rstd_from(src_ap, src_keys, n, scale, dst, dkey):
        np_ = src_ap.shape[0]
        P.act(lambda e: e.activation(out=dst, in_=src_ap, func=AF.Ln, bias=pp[0:np_, PP_EPS:PP_EPS + 1], scale=scale),
              list(src_keys) + ["pp"], [dkey])
        P.act(lambda e: e.activation(out=dst, in_=dst, func=AF.Exp, scale=-0.5), [dkey], [dkey])

    def norm_to_hT(gi):
        gb, gbk = gbt[:], "gbt"
        P.dve(lambda e: e.memset(ss[:, 0:4], 0.0), writes=[("ss", 0), ("ss", 1), ("ss", 2), ("ss", 3)])
        for st in range(4):
            jt_, jk_ = T_()
            junk = jt_[:, 0:512].bitcast(BF16)
            P.act(lambda e, st=st, junk=junk: e.activation(out=junk, in_=xt[:, st, :], func=AF.Square, accum_out=ss[:, st:st + 1]),
                  [("xt", st), ("ss", st)], [("ss", st), jk_])
            rstd_from(ss[:, st:st + 1], [("ss", st)], 1, 1.0 / D, ss[:, 4 + st:5 + st], ("ssr", st))
            P.dve(lambda e, st=st: e.scalar_tensor_tensor(out=hn[:, st, :], in0=xt[:, st, :], scalar=ss[:, 4 + st:5 + st], in1=gb, op0=ALU.mult, op1=ALU.mult),
                  [("xt", st), ("ssr", st), gbk], [("hn", st)])
        for st in range(4):
            b = st % 2
            bb = bank[b][:].bitcast(BF16)
            for kc in range(8):
                P.pe(lambda e, st=st, kc=kc, bb=bb: e.transpose(out=bb[:, kc * 128:(kc + 1) * 128], in_=hn[:, st, kc * 128:(kc + 1) * 128], identity=ident),
                     [("hn", st), "cbf"], [BK[b]])
            P.act(lambda e, st=st, bb=bb: e.activation(out=hT[:, :, st * 128:(st + 1) * 128], in_=bb.rearrange("p (c t) -> p c t", c=8), func=AF.Copy),
                  [BK[b]], ["hT"])
        P.dma("sp", gbt[:], gb_d[1 - gi], writes=["gbt"])

    def rope_phases(name, b, gcol, dsts, dkeys):
        stt = {}

        def A():
            proj_chunk(name, b)
            qraw, qk = T_()
            sq, sk = T_()
            P.act(lambda e: e.activation(out=qraw[:, 0:T], in_=bank[b][:], func=AF.Copy), [BK[b]], [qk])
            P.act(lambda e: e.activation(out=sq[:, 0:T], in_=bank[b][:], func=AF.Square), [BK[b]], [sk])
            stt.update(qraw=qraw, qk=qk, sq=sq, sk=sk)

        def B():
            qraw, qk, sq, sk = stt["qraw"], stt["qk"], stt["sq"], stt["sk"]
            mm(bank[6][:], ones32, sq[:, 0:T], True, True, ["c32", sk], [BK[6]])
            rs, rk = sq, sk
            rstd_from(bank[6][:], [BK[6]], T, 1.0 / 64, rs[:, 0:T], rk)
            P.dve(lambda e: e.scalar_tensor_tensor(out=qraw[:, 0:T], in0=qraw[:, 0:T], scalar=pp[:, gcol:gcol + 1], in1=rs[:, 0:T], op0=ALU.mult, op1=ALU.mult),
                  [qk, rk, "pp"], [qk])

        def C():
            qn, nk, t1, k1 = stt["qraw"], stt["qk"], stt["sq"], stt["sk"]
            mm(bank[7][:], perm32, qn[:, 0:T], True, True, ["c32", nk], [BK[7]])
            P.dve(lambda e: e.tensor_tensor(out=t1[:, 0:T], in0=bank[7][:], in1=sinT[:], op=ALU.mult), [BK[7], "sinT"], [k1])
            P.dve(lambda e: e.tensor_tensor(out=qn[:, 0:T], in0=qn[:, 0:T], in1=cosT[:], op=ALU.mult), [nk, "cosT"], [nk])
            P.dve(lambda e: e.tensor_tensor(out=dsts[0], in0=t1[0:64, 0:T], in1=qn[0:64, 0:T], op=ALU.add), [k1, nk], [dkeys[0]])
            P.dve(lambda e: e.tensor_tensor(out=dsts[1], in0=t1[64:128, 0:T], in1=qn[64:128, 0:T], op=ALU.add), [k1, nk], [dkeys[1]])
        return [A, B, C]

    def combine(h, br, ob, mode):
        row = 3 * h + br
        c, tb = h // 2, 64 * (h % 2)
        rd, rk = T_()
        mm(bank[7][0:64, :], selg[:, row * 64:(row + 1) * 64], gatesT, True, True, ["selg", "gatesT"], [BK[7]])
        bcol = PP_TINY if br == 0 else PP_ZERO
        P.act(lambda e: e.activation(out=rd[0:64, 0:T], in_=bank[ob][64:128, :], func=AF.Ln, bias=pp[0:64, bcol:bcol + 1], scale=1.0), [BK[ob], "pp"], [rk])
        P.act(lambda e: e.activation(out=rd[0:64, 0:T], in_=rd[0:64, 0:T], func=AF.Exp, scale=-1.0), [rk], [rk])
        P.dve(lambda e: e.tensor_tensor(out=rd[0:64, 0:T], in0=rd[0:64, 0:T], in1=bank[7][0:64, :], op=ALU.mult), [rk, BK[7]], [rk])
        pb = 0
        if mode == "first":
            P.dve(lambda e: e.tensor_tensor(out=ybacc[tb:tb + 64, c, :], in0=bank[ob][0:64, :], in1=rd[pb:pb + 64, 0:T], op=ALU.mult),
                  [BK[ob], rk], [("ybacc", h)])
        else:
            t, tk = T_()
            P.dve(lambda e: e.tensor_tensor(out=t[tb:tb + 64, 0:T], in0=bank[ob][0:64, :], in1=rd[pb:pb + 64, 0:T], op=ALU.mult),
                  [BK[ob], rk], [tk])
            if mode == "mid":
                P.dve(lambda e: e.tensor_tensor(out=ybacc[tb:tb + 64, c, :], in0=ybacc[tb:tb + 64, c, :], in1=t[tb:tb + 64, 0:T], op=ALU.add),
                       [tk, ("ybacc", h)], [("ybacc", h)])
            else:
                P.dve(lambda e: e.tensor_tensor(out=ybT[tb:tb + 64, c, :], in0=ybacc[tb:tb + 64, c, :], in1=t[tb:tb + 64, 0:T], op=ALU.add),
                       [tk, ("ybacc", h)], [("ybT", c)])

    def attn_pipeline(items):
        LA = 2
        n = len(items)
        pend = []
        for i in range(n + LA):
            if i < n:
                it = items[i]
                sbk = i % 3
                c0, c1 = it.get("c0", 0), it.get("c1", T)
                has_tri = it.get("tri") is not None
                mm(bank[sbk][:, c0:c1], it["lhsK"], it["rhsQ"][:, c0:c1], True, it["mask"] is None and not has_tri, it["kreads"], [BK[sbk]])
                if it["mask"] is not None:
                    mm(bank[sbk][:, c0:c1], ident, it["mask"], False, True, ["cbf", "masks"], [BK[sbk]])
                if has_tri:
                    tcl = it["tricol"]
                    mm(bank[sbk][:, tcl:tcl + 128], ident, it["tri"], False, True, ["cbf", "masks"], [BK[sbk]])
                pt, pk = PT_()
                P.act(lambda e, pt=pt, sbk=sbk, c0=c0, c1=c1: e.activation(out=pt[:, c0:c1], in_=bank[sbk][:, c0:c1], func=AF.Exp, scale=0.125), [BK[sbk]], [pk])
                pend.append((pt, pk, c0, c1))
            if i >= LA:
                it = items[i - LA]
                pt, pk, c0, c1 = pend[i - LA]
                mm(bank[it["ob"]][:, c0:c1], it["vl"], pt[:, c0:c1], it["first"], it["last"], it["vreads"] + [pk], [BK[it["ob"]]])
                if it.get("extra") is not None:
                    it["extra"](pt, pk)
                if it["post"] is not None:
                    it["post"]()

    out_ops = []
    def rope_tables(tt):
        tok0 = tt * T
        P.dma("sp", posi[:], pos_d[:, tok0:tok0 + T].partition_broadcast(128), writes=["posi"])
        ang, ak = T_()
        P.dve(lambda e: e.tensor_copy(out=ang[:, 0:T], in_=posi[:]), ["posi"], [ak])
        P.dve(lambda e: e.tensor_scalar(out=ang[:, 0:T], in0=ang[:, 0:T], scalar1=pp[:, PP_INV:PP_INV + 1], scalar2=None, op0=ALU.mult), [ak, "pp"], [ak])
        for which in range(2):
            dst, dk = (sinT, "sinT") if which == 0 else (cosT, "cosT")
            a2, a2k = T_()
            shift = 0.0 if which == 0 else math.pi / 2
            P.dve(lambda e, a2=a2, shift=shift: e.tensor_scalar(out=a2[:, 0:T], in0=ang[:, 0:T], scalar1=shift, scalar2=None, op0=ALU.add), [ak], [a2k])
            ki, kik = T_()
            kiv = ki[:, 0:T].bitcast(I32)
            P.dve(lambda e, a2=a2, kiv=kiv: e.tensor_scalar(out=kiv, in0=a2[:, 0:T], scalar1=1.0 / (2 * math.pi), scalar2=None, op0=ALU.mult), [a2k], [kik])
            kf, kfk = T_()
            P.dve(lambda e, kf=kf, kiv=kiv: e.tensor_copy(out=kf[:, 0:T], in_=kiv), [kik], [kfk])
            P.dve(lambda e, kf=kf, a2=a2: e.scalar_tensor_tensor(out=a2[:, 0:T], in0=kf[:, 0:T], scalar=-2 * math.pi, in1=a2[:, 0:T], op0=ALU.mult, op1=ALU.add),
                  [kfk, a2k], [a2k])
            P.dve(lambda e, a2=a2: e.tensor_scalar(out=a2[:, 0:T], in0=a2[:, 0:T], scalar1=3.14159, scalar2=-3.14159, op0=ALU.min, op1=ALU.max), [a2k], [a2k])
            if which == 0:
                P.act(lambda e, a2=a2: e.activation(out=sinT[:], in_=a2[:, 0:T], func=AF.Sin, scale=pp[:, PP_SGN:PP_SGN + 1]), [a2k, "pp"], ["sinT"])
            else:
                P.act(lambda e, a2=a2: e.activation(out=cosT[:], in_=a2[:, 0:T], func=AF.Sin), [a2k], ["cosT"])

    def compute_bias_kv():
        for i in range(2):
            pcol = C_PEK if i == 0 else C_PEV
            for l in range(32):
                mm(bank[7][:, i:i + 1], w1bd[:, i, l * 128:(l + 1) * 128], cbf[:, pcol + l:pcol + l + 1], l == 0, l == 31,
                   ["w1bd", "cbf"], [BK[7]])
        P.act(lambda e: e.activation(out=biasKV[:], in_=bank[7][:, 0:2], func=AF.Copy), [BK[7]], ["biasKV"])

    def do_tile(tt):
        tok0 = tt * T
        if tt > 0:
            for st in range(4):
                P.dma("pool", xt[:, st, :], x_d[tok0 + st * 128:tok0 + (st + 1) * 128, :], writes=[("xt", st)])
        norm_to_hT(0)
        mixst = {}

        def mixerA1(ci):
            proj_chunk(f"ac{ci}", 3)
            tc, tck = T_()
            P.act(lambda e: e.activation(out=tc[:, 0:T], in_=bank[3][:], func=AF.Copy), [BK[3]], [tck])
            proj_chunk(f"ax{ci}", 4)
            P.dve(lambda e: e.tensor_tensor(out=uh[:, ci, 2:514], in0=bank[4][:], in1=tc[:, 0:T], op=ALU.mult), [BK[4], tck], [("uh", ci)])
            cv, cvk = tc, tck
            wc = PP_CA + ci * 3
            P.dve(lambda e: e.tensor_scalar(out=cv[:, 0:T], in0=uh[:, ci, 0:512], scalar1=pp[:, wc:wc + 1], scalar2=None, op0=ALU.mult),
                  [("uh", ci), "pp"], [cvk])
            for k in (1, 2):
                P.dve(lambda e, k=k: e.scalar_tensor_tensor(out=cv[:, 0:T], in0=uh[:, ci, k:k + 512], scalar=pp[:, wc + k:wc + k + 1],
                                                            in1=cv[:, 0:T], op0=ALU.mult, op1=ALU.add), [("uh", ci), "pp", cvk], [cvk])
            P.dve(lambda e: e.tensor_copy(out=uh[:, ci, 0:2], in_=uh[:, ci, 512:514]), [("uh", ci)], [("uh", ci)])
            mixst[ci] = (cv, cvk)

        def mixerA2(ci):
            cv, cvk = mixst[ci]
            proj_chunk(f"ab{ci}", 5)
            P.dve(lambda e: e.tensor_tensor(out=yaT[:, ci, :], in0=bank[5][:], in1=cv[:, 0:T], op=ALU.mult), [BK[5], cvk], [("yaT", ci)])

        def cmp_in(i, nm):
            proj_chunk(nm, 4)
            P.act(lambda e: e.activation(out=kvh[:, i, 16:528], in_=bank[4][:], func=AF.Copy), [BK[4]], [("kvh", i)])

        def vproj(nm, vt, vk, kt0):
            w, wkey = wload(nm)
            for st in range(4):
                for kc in range(8):
                    mm(bank[5][:, st * 128:(st + 1) * 128], hT[:, kc, st * 128:(st + 1) * 128], w[:, kc * 128:(kc + 1) * 128], kc == 0, kc == 7,
                       [wkey, "hT"], [BK[5]])
            P.act(lambda e: e.activation(out=vt[:, kt0:kt0 + 4, :, 0:64], in_=bank[5][:].rearrange("p (s g d) -> p s g d", s=4, g=2),
                                         func=AF.Copy), [BK[5]], [(vk, tt)])

        def gate_all():
            w, wkey = wload("gtT")
            for kc in range(8):
                mm(bank[4][0:24, :], w[:, kc * 128:kc * 128 + 24], hT[:, kc, :], kc == 0, kc == 7, [wkey, "hT"], [BK[4]])
            P.act(lambda e: e.activation(out=gatesT, in_=bank[4][0:24, :], func=AF.Sigmoid), [BK[4]], ["gatesT"])

        wbase = (tt % 2) * 512
        rq = [rope_phases(f"q{c}", c % 3, PP_GQ, [Qst[0:64, 2 * c, :], Qst[0:64, 2 * c + 1, :]], [("Q", 2 * c), ("Q", 2 * c + 1)]) for c in range(4)]
        rks = rope_phases("ksl", 1, PP_GK + 1, [kselst[0:64, 0, tok0:tok0 + T], kselst[0:64, 1, tok0:tok0 + T]], [("ksel", tt), ("ksel", tt)])
        rkw = rope_phases("kwn", 2, PP_GK + 2, [kwinT[0:64, 0, wbase:wbase + T], kwinT[0:64, 1, wbase:wbase + T]], ["kwinT", "kwinT"])
        A_, B_, C_ = 0, 1, 2
        ropes = rq + [rks, rkw]
        f1 = [lambda: mixerA1(0), lambda: mixerA1(1), lambda: mixerA1(2), lambda: mixerA1(3),
              lambda: cmp_in(0, "kc"), lambda: vproj("vsl", vsel, "vsel", 4 * tt)]
        f2 = [lambda: (gate_all(), mixerA2(0)), lambda: mixerA2(1), lambda: mixerA2(2), lambda: mixerA2(3),
              lambda: cmp_in(1, "vc"), lambda: vproj("vwn", vwin, "vwin", (4 * tt) % 8)]
        for k, r in enumerate(ropes):
            r[A_]()
            f1[k]()
            r[B_]()
            f2[k]()
            r[C_]()
        if tt == 0:
            ensure_cast(NCHP // 4, pace=max(paced))
        if tt + 1 < NT:
            rope_tables(tt + 1)
        if tt == 0:
            compute_bias_kv()
        s0 = 32 * tt
        jt_new, po = s0 // 128, s0 % 128
        for i in range(2):
            for l in range(32):
                mm(bank[6][:, i * 32:(i + 1) * 32], w1bd[:, i, l * 128:(l + 1) * 128], kvh[:, i, l:l + 497:16], l == 0, l == 31,
                   ["w1bd", ("kvh", i)], [BK[6]])
            P.act(lambda e, i=i: e.activation(out=hid[:, i, :], in_=bank[6][:, i * 32:(i + 1) * 32], func=AF.Silu, bias=biasKV[:, i:i + 1], scale=1.0),
                  [BK[6], "biasKV"], [("hid", i)])
            P.dve(lambda e, i=i: e.tensor_copy(out=kvh[:, i, 0:16], in_=kvh[:, i, 512:528]), [("kvh", i)], [("kvh", i)])
        mm(bank[7][:, 0:32], cbf[:, C_W2K:C_W2K + 128], hid[:, 0, :], True, True, ["cbf", ("hid", 0)], [BK[7]])
        P.act(lambda e: e.activation(out=small[:, 0, :], in_=bank[7][:, 0:32], func=AF.Copy), [BK[7]], ["small0"])
        P.act(lambda e: e.activation(out=small[:, 1, :], in_=bank[7][:, 0:32], func=AF.Square), [BK[7]], ["small1"])
        mm(bank[7][:, 32:64], ones32, small[:, 1, :], True, True, ["c32", "small1"], [BK[7]])
        rstd_from(bank[7][:, 32:64], [BK[7]], 32, 1.0 / 64, small[:, 2, :], "small2")
        P.dve(lambda e: e.scalar_tensor_tensor(out=small[:, 3, :], in0=small[:, 0, :], scalar=pp[:, PP_GK:PP_GK + 1], in1=small[:, 2, :], op0=ALU.mult, op1=ALU.mult),
              ["small0", "small2", "pp"], ["small3"])
        for g in range(2):
            P.dve(lambda e, g=g: e.tensor_copy(out=kcmpT[0:64, g, s0:s0 + 32], in_=small[g * 64:(g + 1) * 64, 3, :]), ["small3"], ["kcmpT"])
        mm(bank[7][0:32, 128:256], hid[:, 1, :], cbf[:, C_W2V:C_W2V + 128], True, True, ["cbf", ("hid", 1)], [BK[7]])
        P.act(lambda e: e.activation(out=vcmp[po:po + 32, jt_new, :, 0:64], in_=bank[7][0:32, 128:256].rearrange("p (g d) -> p g d", g=2), func=AF.Copy),
              [BK[7]], ["vcmp"])
        if tt == 0:
            P.dve(lambda e: e.memset(vcmp[0:1, 0, :, :], 0.0), writes=["vcmp"])
        jts = [0] if tt < 4 else [0, 1]
        for g in range(2):
            items = []
            for hh in range(4):
                h = 4 * g + hh
                for jt in jts:
                    delta = tt - 4 * jt
                    mask = CMPM[:, 512 * delta:512 * delta + 512] if 0 <= delta <= 3 else None
                    ob = 3 + (h % 2)
                    ib = 5 + (h % 2)

                    def extra(pt, pk, jt=jt, ib=ib):
                        for st in range(4):
                            mm(bank[ib][:, st * 128:st * 128 + 65], pt[:, st * 128:(st + 1) * 128], ovl[:, jt, :], jt == 0, jt == jts[-1], [pk, "cbf"], [BK[ib]])

                    def post(h=h, hh=hh, g=g, ob=ob, ib=ib):
                        combine(h, 0, ob, "first")
                        iv = bank[ib][:].rearrange("p (s c) -> p s c", s=4)
                        P.dve(lambda e: e.tensor_scalar(out=rden4[:], in0=iv[:, :, 64], scalar1=1e-30, scalar2=None, op0=ALU.add), [BK[ib]], ["rden4"])
                        P.dve(lambda e: e.reciprocal(out=rden4[:], in_=rden4[:]), ["rden4"], ["rden4"])
                        for st in range(4):
                            if hh == 0:
                                P.dve(lambda e, st=st: e.tensor_scalar(out=impacc[:, g, st, :], in0=iv[:, st, 0:64], scalar1=rden4[:, st:st + 1], scalar2=None, op0=ALU.mult),
                                      [BK[ib], "rden4"], [("impacc", g)])
                            else:
                                P.dve(lambda e, st=st: e.scalar_tensor_tensor(out=impacc[:, g, st, :], in0=iv[:, st, 0:64], scalar=rden4[:, st:st + 1], in1=impacc[:, g, st, :],
                                                                              op0=ALU.mult, op1=ALU.add), [BK[ib], "rden4", ("impacc", g)], [("impacc", g)])
                    items.append(dict(lhsK=kcmpT[0:64, g, jt * 128:(jt + 1) * 128], rhsQ=Qst[0:64, h, :], kreads=["kcmpT", ("Q", h)], mask=mask,
                                      vl=vcmp[:, jt, g, :], vreads=["vcmp"], ob=ob, first=jt == 0, last=jt == jts[-1], extra=extra,
                                      post=post if jt == jts[-1] else None))
            attn_pipeline(items)
        def sel_chain(g, st):
            c0 = 8 * tt + 2 * st
            lo = 64 - c0
            P.dve(lambda e: e.tensor_tensor(out=adj[:], in0=impacc[:, g, st, :], in1=c32[:, C_WM + lo:C_WM + lo + 64], op=ALU.mult), [("impacc", g), "c32"], ["adj"])
            P.dve(lambda e: e.tensor_tensor(out=adj[:], in0=adj[:], in1=c32[:, C_WB + lo:C_WB + lo + 64], op=ALU.add), ["adj", "c32"], ["adj"])
            P.dve(lambda e: e.memset(adj[:, 0:1], 1e9), ["adj"], ["adj"])
            P.dve(lambda e: e.max(out=m8[:, 0:8], in_=adj[:]), ["adj"], ["m8"])
            P.dve(lambda e: e.match_replace(out=wk[:], in_to_replace=m8[:, 0:8], in_values=adj[:], imm_value=-3e9), ["adj", "m8"], ["wk"])
            P.dve(lambda e: e.max(out=m8[:, 8:16], in_=wk[:]), ["wk", "m8"], ["m8"])
            P.dve(lambda e: e.tensor_scalar(out=smT[:, g, st, 64:128], in0=adj[:], scalar1=m8[:, 15:16], scalar2=1.0, op0=ALU.is_ge, op1=ALU.subtract),
                  ["adj", "m8"], [("smT", g)])

        if tt == 0 and "yc" in dbg:
            P.dma("sp", dbg_d["yc"], ybacc[:, 0, :], reads=[("ybacc", 0), ("ybacc", 1)])
        sel_chain(0, 0)
        sel_chain(0, 1)
        items = []
        for h in range(8):
            g = h // 4
            owb = 3 + (h % 2)
            kts = [kt for kt in range(4 * tt - 4, 4 * tt + 4) if kt >= 0]
            kts = [4 * tt] + [kt for kt in kts if kt != 4 * tt]
            for kt in kts:
                d = kt - 4 * tt
                slot = kt % 8
                if d >= 0:
                    c0, c1, tri, tcl = 128 * d, T, TRI_LE, 128 * d
                else:
                    dd = d + 4
                    c0, c1, tri, tcl = 0, 128 * (dd + 1), TRI_GT, 128 * dd
                items.append(dict(lhsK=kwinT[0:64, g, slot * 128:(slot + 1) * 128], rhsQ=Qst[0:64, h, :], kreads=["kwinT", ("Q", h)], mask=None,
                                  c0=c0, c1=c1, tri=tri, tricol=tcl,
                                  vl=vwin[:, slot, g, :], vreads=[("vwin", kt // 4), "vwin1"], ob=owb, first=kt == kts[0], last=kt == kts[-1],
                                  post=(lambda h=h, owb=owb: (combine(h, 2, owb, "mid"), sel_chain((h + 2) // 4, (h + 2) % 4) if h + 2 < 8 else None)) if kt == kts[-1] else None))
        attn_pipeline(items)
        for g in range(2):
            sbk = 7 - g
            bb = bank[sbk][:].bitcast(BF16)
            for st in range(4):
                P.pe(lambda e, g=g, st=st, bb=bb: e.transpose(out=bb[:, st * 128:(st + 1) * 128], in_=smT[:, g, st, :], identity=ident), [("smT", g), "cbf"], [BK[sbk]])
            for hh in range(4):
                h = 4 * g + hh
                P.act(lambda e, h=h, bb=bb: e.activation(out=Qst[64:128, h, :], in_=bb[64:128, 0:512], func=AF.Copy, scale=-NEGB), [BK[sbk]], [("Qs", h)])
        items = []
        for h in range(8):
            g = h // 4
            osb = 5 + (h % 2)
            nkt = 4 * tt + 4
            order = ([4 * tt] + [kt for kt in range(nkt) if kt != 4 * tt]) if tt == 0 else list(range(nkt))
            for kt in order:
                d = kt - 4 * tt
                if d >= 0:
                    c0, c1, tri, tcl = 128 * d, T, TRI_LE, 128 * d
                else:
                    c0, c1, tri, tcl = 0, T, None, 0
                items.append(dict(lhsK=kselst[:, g, kt * 128:(kt + 1) * 128], rhsQ=Qst[:, h, :], kreads=[("ksel", kt // 4), "kselst_e", ("Q", h), ("Qs", h)], mask=None,
                                  c0=c0, c1=c1, tri=tri, tricol=tcl,
                                  vl=vsel[:, kt, g, :], vreads=[("vsel", kt // 4), "vsel1"], ob=osb, first=kt == order[0], last=kt == order[-1],
                                  post=(lambda h=h, osb=osb: combine(h, 1, osb, "last")) if kt == order[-1] else None))
        attn_pipeline(items)
        if tt == 0 and "ys" in dbg:
            P.dma("sp", dbg_d["ys"], ybacc[:, 0, :], reads=[("ybacc", 0), ("ybacc", 1)])
        for dc in range(8):
            w, wkey = wload(f"wab{dc}")
            o4 = 4 * (dc % 2)
            for kc in range(4):
                mm(bank[o4][:], w[:, kc * 128:(kc + 1) * 128], yaT[:, kc, :], kc == 0, kc == 3, [wkey, ("yaT", kc)], [BK[o4]])
            for kc in range(4):
                mm(bank[o4 + 1][:], w[:, (4 + kc) * 128:(5 + kc) * 128], ybT[:, kc, :], kc == 0, kc == 3, [wkey, ("ybT", kc)], [BK[o4 + 1]])
            proj_chunk(f"gmA{dc}", o4 + 2)
            proj_chunk(f"gmB{dc}", o4 + 3)
            sA, sAk = T_()
            sB, sBk = T_()
            P.act(lambda e, sA=sA, o4=o4: e.activation(out=sA[:, 0:T], in_=bank[o4 + 2][:], func=AF.Sigmoid), [BK[o4 + 2]], [sAk])
            P.act(lambda e, sB=sB, o4=o4: e.activation(out=sB[:, 0:T], in_=bank[o4 + 3][:], func=AF.Sigmoid), [BK[o4 + 3]], [sBk])
            P.dve(lambda e, sA=sA, o4=o4: e.tensor_tensor(out=sA[:, 0:T], in0=bank[o4][:], in1=sA[:, 0:T], op=ALU.mult), [BK[o4], sAk], [sAk])
            P.dve(lambda e, sB=sB, o4=o4: e.tensor_tensor(out=sB[:, 0:T], in0=bank[o4 + 1][:], in1=sB[:, 0:T], op=ALU.mult), [BK[o4 + 1], sBk], [sBk])
            P.dve(lambda e, sA=sA, sB=sB, dc=dc: e.tensor_tensor(out=mixT[:, dc, :], in0=sA[:, 0:T], in1=sB[:, 0:T], op=ALU.add), [sAk, sBk], ["mixT"])
        gi = 0
        for half in range(2):
            chunks = [wload(f"wo{half}_{kq}") for kq in range(4)]
            for st in range(4):
                b = gi % 4
                gi += 1
                for kc in range(8):
                    w, wkey = chunks[kc // 2]
                    mm(bank[b][:], mixT[:, kc, st * 128:(st + 1) * 128], w[:, (kc % 2) * 512:(kc % 2) * 512 + 512], kc == 0, kc == 7, [wkey, "mixT"], [BK[b]])
                P.dve(lambda e, st=st, half=half, b=b: e.tensor_tensor(out=xt[:, st, half * 512:(half + 1) * 512], in0=bank[b][:], in1=xt[:, st, half * 512:(half + 1) * 512], op=ALU.add),
                      [BK[b], ("xt", st)], [("xt", st)])
        norm_to_hT(1)
        for fc in range(NFC):
            gb = 2 * (fc % 4)
            ub = gb + 1
            proj_chunk(f"fg{fc}", gb)
            proj_chunk(f"fu{fc}", ub)
            gs, gk = T_()
            P.act(lambda e, gs=gs, gb=gb: e.activation(out=gs[:, 2:514], in_=bank[gb][:], func=AF.Copy), [BK[gb]], [gk])
            us = None
            if fc >= NFC - 4:
                us, usk = T_()
                P.act(lambda e, us=us, ub=ub: e.activation(out=us[:, 0:T], in_=bank[ub][:], func=AF.Copy), [BK[ub]], [usk])
            P.dve(lambda e, gs=gs, fc=fc: e.tensor_copy(out=gs[:, 0:2], in_=ghalo[:, fc, :]), [("ghalo", fc), gk], [gk])
            cv, cvk = T_()
            wc = PP_FW + fc * 3
            P.dve(lambda e, cv=cv, gs=gs, wc=wc: e.tensor_scalar(out=cv[:, 0:T], in0=gs[:, 0:512], scalar1=pp[:, wc:wc + 1], scalar2=None, op0=ALU.mult), [gk, "pp"], [cvk])
            for k in (1, 2):
                P.dve(lambda e, cv=cv, gs=gs, wc=wc, k=k: e.scalar_tensor_tensor(out=cv[:, 0:T], in0=gs[:, k:k + 512], scalar=pp[:, wc + k:wc + k + 1], in1=cv[:, 0:T],
                                                                                   op0=ALU.mult, op1=ALU.add), [gk, "pp", cvk], [cvk])
            P.dve(lambda e, gs=gs, fc=fc: e.tensor_copy(out=ghalo[:, fc, :], in_=gs[:, 512:514]), [gk], [("ghalo", fc)])
            P.act(lambda e, cv=cv, fc=fc: e.activation(out=cv[:, 0:T], in_=cv[:, 0:T], func=AF.Silu, bias=pp[:, PP_FB + fc:PP_FB + fc + 1], scale=1.0), [cvk, "pp"], [cvk])
            if us is not None:
                P.dve(lambda e, cv=cv, fc=fc, us=us: e.tensor_tensor(out=actT[:, fc, :], in0=us[:, 0:T], in1=cv[:, 0:T], op=ALU.mult), [usk, cvk], [("actT", fc)])
            else:
                P.dve(lambda e, cv=cv, fc=fc, ub=ub: e.tensor_tensor(out=actT[:, fc, :], in0=bank[ub][:], in1=cv[:, 0:T], op=ALU.mult), [BK[ub], cvk], [("actT", fc)])
        for fc in range(NFC):
            w, wkey = wload(f"wd{fc}")
            for st in ((2, 3, 0, 1) if fc == 0 else (0, 1, 2, 3)):
                for half in range(2):
                    b = 2 * st + half
                    mm(bank[b][:], actT[:, fc, st * 128:(st + 1) * 128], w[:, half * 512:(half + 1) * 512], fc == 0, fc == NFC - 1, [wkey, ("actT", fc)], [BK[b]])
        for st in range(4):
            for half in range(2):
                b = 2 * st + half
                P.dve(lambda e, st=st, half=half, b=b: e.tensor_tensor(out=xt[:, st, half * 512:(half + 1) * 512], in0=bank[b][:], in1=xt[:, st, half * 512:(half + 1) * 512], op=ALU.add),
                      [BK[b], ("xt", st)], [("xt", st)])
            out_ops.append(P.dma("pool", out_d[tok0 + st * 128:tok0 + (st + 1) * 128, :], xt[:, st, :], reads=[("xt", st)]))
    rope_tables(0)
    for tt_ in range(NT):
        do_tile(tt_)
    P.emit(final_wait_ops=[o for o in P.ops if o.dma])
    P.close()
    return nc, P.stats


_CACHE = {}


def kernel(**inputs):
    lay = host_layout(inputs)
    if "nc" not in _CACHE:
        _CACHE["nc"] = build()[0]
    nc = _CACHE["nc"]
    x = np.asarray(inputs["x"], np.float32)
    pos = np.asarray(inputs["positions"], np.int32)
    in_maps = []
    for b in range(8):
        m = dict(lay)
        m["x"] = np.ascontiguousarray(x[b])
        m["pos"] = np.ascontiguousarray(pos[b:b + 1])
        in_maps.append(m)
    res = run_bass_kernel_spmd(nc, in_maps, core_ids=list(range(8)))
    return np.stack([np.asarray(r["out"], np.float32) for r in res.results], axis=0)
```
